# Optimizing a Trainium2 kernel written in Bass

```python
import math
import jax, jax.numpy as jnp
from jax import lax
import numpy as np

D_MODEL = 1024
BATCH = 8
SEQ = 4096
DEPTH = 1

N_HEADS = 8
HEAD_DIM = 128
ATTN_WIDTH = N_HEADS * HEAD_DIM
CONV_WIDTH = 1024
CONV_K = 3
MOBA_BLOCK = 256
MOBA_TOPK = 3
Q_CHUNK = 8
N_BUCKETS = 32
MAX_DISTANCE = 128
EPS = 1e-6
IN_WIDTHS = (ATTN_WIDTH, ATTN_WIDTH, ATTN_WIDTH, ATTN_WIDTH,
             CONV_WIDTH, CONV_WIDTH, CONV_WIDTH, CONV_WIDTH,
             D_MODEL, D_MODEL)
IN_COLS = sum(IN_WIDTHS)
IN_OFFSETS = tuple(int(o) for o in np.cumsum(IN_WIDTHS)[:-1])

kernel_name = "moba_shortconv_gated_hybrid_layer"


def rmsnorm(x, g):
    x32 = x.astype(jnp.float32)
    y = x32 * lax.rsqrt(jnp.mean(x32 * x32, axis=-1, keepdims=True) + EPS)
    return (y * g.astype(jnp.float32)).astype(x.dtype)


def t5_bucket(dist):
    n = jnp.maximum(dist, 0)
    max_exact = N_BUCKETS // 2
    nf = jnp.maximum(n, 1).astype(jnp.float32)
    large = max_exact + (jnp.log(nf / max_exact) / math.log(MAX_DISTANCE / max_exact)
                         * (N_BUCKETS - max_exact)).astype(jnp.int32)
    large = jnp.minimum(large, N_BUCKETS - 1)
    return jnp.where(n < max_exact, n, large)


def moba_attention(q, k, v, rel_bias):
    bsz, n_h, seq, hd = q.shape
    s_pad = -(-seq // MOBA_BLOCK) * MOBA_BLOCK
    padw = ((0, 0), (0, 0), (0, s_pad - seq), (0, 0))
    q32 = jnp.pad(q.astype(jnp.float32), padw) * (hd ** -0.5)
    nb = s_pad // MOBA_BLOCK
    kb = jnp.pad(k.astype(jnp.float32), padw).reshape(bsz, n_h, nb, MOBA_BLOCK, hd)
    vb = jnp.pad(v.astype(jnp.float32), padw).reshape(bsz, n_h, nb, MOBA_BLOCK, hd)
    bias_hb = rel_bias.astype(jnp.float32).T
    n_sel = min(MOBA_TOPK, nb - 1)
    n_chunks = s_pad // Q_CHUNK
    q_ch = q32.reshape(bsz, n_h, n_chunks, Q_CHUNK, hd).transpose(2, 0, 1, 3, 4)
    starts = jnp.arange(n_chunks, dtype=jnp.int32) * Q_CHUNK

    if n_sel > 0:
        k_mean = kb.mean(axis=3)
        scores = jnp.einsum('bhsd,bhnd->bhsn', q32, k_mean)
        q_blk = jnp.arange(s_pad) // MOBA_BLOCK
        fully_past = jnp.arange(nb)[None, :] < q_blk[:, None]
        scores = jnp.where(fully_past, scores, -jnp.inf)
        _, sel = lax.top_k(scores, n_sel)
        sel = sel.astype(jnp.int32)
    else:
        sel = jnp.zeros((bsz, n_h, s_pad, 0), jnp.int32)
    sel_ch = sel.reshape(bsz, n_h, n_chunks, Q_CHUNK, n_sel).transpose(2, 0, 1, 3, 4)

    b_idx = jnp.arange(bsz)[:, None, None, None]
    h_idx = jnp.arange(n_h)[None, :, None, None]
    offs = jnp.arange(MOBA_BLOCK, dtype=jnp.int32)

    def chunk(args):
        qc, selc, c0 = args
        qpos = c0 + jnp.arange(Q_CHUNK, dtype=jnp.int32)
        own = c0 // MOBA_BLOCK
        k_own = kb[:, :, own]
        v_own = vb[:, :, own]
        dist_own = qpos[:, None] - (own * MOBA_BLOCK + offs)[None, :]
        lg_own = jnp.einsum('bhqd,bhkd->bhqk', qc, k_own) + bias_hb[:, t5_bucket(dist_own)]
        lg_own = jnp.where(dist_own >= 0, lg_own, -jnp.inf)
        k_sel = kb[b_idx, h_idx, selc]
        v_sel = vb[b_idx, h_idx, selc]
        dist_sel = qpos[:, None, None] - (selc[..., None] * MOBA_BLOCK + offs)
        lg_sel = (jnp.einsum('bhqd,bhqnkd->bhqnk', qc, k_sel)
                  + bias_hb[h_idx[..., None], t5_bucket(dist_sel)])
        lg_sel = jnp.where((selc < own)[..., None], lg_sel, -jnp.inf)
        logits = jnp.concatenate(
            [lg_sel.reshape(bsz, n_h, Q_CHUNK, n_sel * MOBA_BLOCK), lg_own], axis=-1)
        p = jax.nn.softmax(logits, axis=-1)
        p_sel = p[..., :n_sel * MOBA_BLOCK].reshape(bsz, n_h, Q_CHUNK, n_sel, MOBA_BLOCK)
        p_own = p[..., n_sel * MOBA_BLOCK:]
        return (jnp.einsum('bhqnk,bhqnkd->bhqd', p_sel, v_sel)
                + jnp.einsum('bhqk,bhkd->bhqd', p_own, v_own))

    out = lax.map(chunk, (q_ch, sel_ch, starts))
    out = out.transpose(1, 2, 0, 3, 4).reshape(bsz, n_h, s_pad, hd)[:, :, :seq]
    return out.astype(q.dtype)


def short_conv(cb, cc, cx, conv_w):
    u = cc * cx
    y = lax.conv_general_dilated(
        u, conv_w[:, None, :].astype(u.dtype), window_strides=(1,),
        padding=[(CONV_K - 1, 0)], dimension_numbers=('NWC', 'WIO', 'NWC'),
        feature_group_count=CONV_WIDTH)
    return cb * y


def split_heads(t):
    b, s, _ = t.shape
    return t.reshape(b, s, N_HEADS, HEAD_DIM).transpose(0, 2, 1, 3)


def hybrid_layer(x, c, norm_g, w_ada, b_ada, w_in, conv_w, w_o_attn, w_o_conv, w_out, rel_bias):
    bsz, seq, _ = x.shape
    mod = jax.nn.silu(c) @ w_ada + b_ada
    shift, scale, gate = jnp.split(mod, 3, axis=-1)
    h = rmsnorm(x, norm_g) * (1 + scale[:, None, :]) + shift[:, None, :]
    proj = h @ w_in
    q, k, v, g_attn, cb, cc, cx, g_conv, m_attn, m_conv = jnp.split(proj, IN_OFFSETS, axis=-1)
    attn = moba_attention(split_heads(q), split_heads(k), split_heads(v), rel_bias)
    attn = attn.transpose(0, 2, 1, 3).reshape(bsz, seq, ATTN_WIDTH)
    y_attn = (attn * jax.nn.silu(g_attn)) @ w_o_attn
    y_conv = (short_conv(cb, cc, cx, conv_w) * jax.nn.silu(g_conv)) @ w_o_conv
    merged = jax.nn.sigmoid(m_attn) * y_attn + jax.nn.sigmoid(m_conv) * y_conv
    return x + gate[:, None, :] * (merged @ w_out)


def setup_inputs(seed: int = 0) -> dict:
    key = jax.random.key(seed)
    ks = jax.random.split(key, 13)
    nrm = jax.random.normal
    f32 = jnp.float32
    return {
        "x": nrm(ks[0], (BATCH, SEQ, D_MODEL), f32),
        "c": nrm(ks[1], (BATCH, D_MODEL), f32),
        "norm_g": 1.0 + 0.05 * nrm(ks[2], (DEPTH, D_MODEL), f32),
        "w_ada": 0.5 * D_MODEL ** -0.5 * nrm(ks[3], (DEPTH, D_MODEL, 3 * D_MODEL), f32),
        "b_ada": 0.1 * nrm(ks[4], (DEPTH, 3 * D_MODEL), f32),
        "w_in": D_MODEL ** -0.5 * nrm(ks[5], (DEPTH, D_MODEL, IN_COLS), f32),
        "conv_w": CONV_K ** -0.5 * nrm(ks[6], (DEPTH, CONV_K, CONV_WIDTH), f32),
        "w_o_attn": ATTN_WIDTH ** -0.5 * nrm(ks[7], (DEPTH, ATTN_WIDTH, D_MODEL), f32),
        "w_o_conv": CONV_WIDTH ** -0.5 * nrm(ks[8], (DEPTH, CONV_WIDTH, D_MODEL), f32),
        "w_out": D_MODEL ** -0.5 * nrm(ks[9], (DEPTH, D_MODEL, D_MODEL), f32),
        "rel_bias": 0.5 * nrm(ks[10], (N_BUCKETS, N_HEADS), f32),
        "final_g": 1.0 + 0.05 * nrm(ks[11], (D_MODEL,), f32),
    }


def reference(x, c, norm_g, w_ada, b_ada, w_in, conv_w, w_o_attn, w_o_conv, w_out, rel_bias, final_g):
    for layer in range(DEPTH):
        x = hybrid_layer(x, c, norm_g[layer], w_ada[layer], b_ada[layer], w_in[layer],
                         conv_w[layer], w_o_attn[layer], w_o_conv[layer], w_out[layer], rel_bias)
    return rmsnorm(x, final_g)
```

```python
import bisect
import numpy as np
from contextlib import ExitStack
import concourse.bass as bass
import concourse.mybir as mybir
from concourse.bass_utils import run_bass_kernel_spmd

F32 = mybir.dt.float32
BF16 = mybir.dt.bfloat16
AF = mybir.ActivationFunctionType
ALU = mybir.AluOpType
AX = mybir.AxisListType

D = 1024
KC = 8
H = 8
HD = 128
NEG = -30000.0
EPS = 1e-6
COMPUTE = ("pe", "act", "dve", "pool")
STRICT = ("act", "dve", "pool")
POOL_DEN = 0


class Prog:
    def __init__(self, nc):
        self.nc = nc
        self.ops = {e: [] for e in ("pe", "act", "dve", "pool", "sp")}
        self.sigseqs = {e: [] for e in COMPUTE}
        self.waited = {e: {} for e in self.ops}
        self.state = {}
        self.dma_cnt = {}

    def _resolve(self, dep):
        if dep[0] == "dma":
            return (("dma", dep[1]), dep[2])
        _, eng, seq = dep
        lst = self.sigseqs[eng]
        i = bisect.bisect_left(lst, seq)
        if i == len(lst):
            last = len(self.ops[eng]) - 1
            while self.ops[eng][last]["fn"] is None or self.ops[eng][last]["dma"] is not None:
                last -= 1
            assert last >= seq, (eng, seq, last)
            rec = self.ops[eng][last]
            assert not rec["signal"]
            rec["signal"] = True
            lst.append(last)
            i = len(lst) - 1
        return (("eng", eng), i + 1)

    def _collect(self, eng, reads, writes):
        deps = []
        for k in reads:
            st = self.state.get(k)
            if st and st[0] is not None:
                deps.append(st[0])
        for k in writes:
            st = self.state.get(k)
            if st:
                if st[0] is not None:
                    deps.append(st[0])
                deps.extend(st[1].values())
        waits = {}
        for d in deps:
            if d[0] == "eng" and d[1] == eng and eng not in STRICT:
                continue
            semkey, val = self._resolve(d)
            if self.waited[eng].get(semkey, 0) >= val:
                continue
            if waits.get(semkey, 0) < val:
                waits[semkey] = val
        for semkey, val in waits.items():
            self.waited[eng][semkey] = val
        return list(waits.items())

    def _update(self, me, reads, writes):
        for k in reads:
            st = self.state.setdefault(k, [None, {}])
            st[1][(me[0], me[1])] = me
        for k in writes:
            self.state[k] = [me, {}]

    def op(self, eng, fn, reads=(), writes=(), signal=True):
        waits = self._collect(eng, reads, writes)
        seq = len(self.ops[eng])
        self.ops[eng].append({"fn": fn, "waits": waits, "signal": signal, "dma": None})
        if signal:
            self.sigseqs[eng].append(seq)
        self._update(("eng", eng, seq), reads, writes)

    def dma(self, queue, fn, sem, reads=(), writes=(), inc=16):
        waits = self._collect(queue, reads, writes)
        val = self.dma_cnt.get(sem, 0) + inc
        self.dma_cnt[sem] = val
        self.ops[queue].append({"fn": fn, "waits": waits, "signal": False, "dma": (sem, inc)})
        self._update(("dma", sem, val), reads, writes)

    def barrier(self):
        targets = []
        for e in COMPUTE:
            n = len(self.ops[e])
            last = n - 1
            while last >= 0 and (self.ops[e][last]["fn"] is None or self.ops[e][last]["dma"] is not None):
                last -= 1
            if last >= 0:
                targets.append(self._resolve(("eng", e, last)))
        for name, val in self.dma_cnt.items():
            targets.append((("dma", name), val))
        for e in self.ops:
            waits = []
            for semkey, val in targets:
                if semkey == ("eng", e):
                    continue
                if self.waited[e].get(semkey, 0) >= val:
                    continue
                self.waited[e][semkey] = val
                waits.append((semkey, val))
            if waits:
                self.ops[e].append({"fn": None, "waits": waits, "signal": False, "dma": None})
        self.state = {}

    def emit(self, stack):
        nc = self.nc
        sems = {}
        for e in COMPUTE:
            sems[("eng", e)] = stack.enter_context(nc.semaphore(f"s_{e}"))
        for name in self.dma_cnt:
            sems[("dma", name)] = stack.enter_context(nc.semaphore(f"d_{name}"))
        block = stack.enter_context(nc.Block())
        engmap = {"pe": "tensor", "act": "scalar", "dve": "vector", "pool": "gpsimd", "sp": "sync"}

        def make(ename):
            recs = self.ops[ename]

            def body(eng):
                for rec in recs:
                    for semkey, val in rec["waits"]:
                        eng.wait_ge(sems[semkey], val)
                    if rec["fn"] is None:
                        continue
                    ins = rec["fn"](eng)
                    if rec["dma"] is not None:
                        ins.then_inc(sems[("dma", rec["dma"][0])], rec["dma"][1])
                    elif rec["signal"]:
                        ins.then_inc(sems[("eng", ename)], 1)
            return body

        for ename, attr in engmap.items():
            if self.ops[ename]:
                getattr(block, attr)(make(ename))


class Arena:
    def __init__(self, ap, words):
        self.ap = ap
        self.words = words
        self.off = 0

    def reset(self):
        self.off = 0

    def alloc(self, free_elems, dtype):
        nbytes = free_elems * (2 if dtype == BF16 else 4)
        w = (nbytes + 3) // 4
        w = (w + 7) // 8 * 8
        assert self.off + w <= self.words, ("arena overflow", self.off, w, self.words)
        v = self.ap[:, self.off:self.off + w]
        self.off += w
        if dtype == BF16:
            v = v.bitcast(BF16)
        return v[:, 0:free_elems]


def build_program(S, stop_after=None):
    NT = S // 128
    NC5 = S // 512
    NB = S // 256
    NCH = S // 256
    assert S % 1024 == 0 and NB <= 16
    nc = bass.Bass("TRN2", target_bir_lowering=False)

    def din(name, shape, dt=F32):
        return nc.dram_tensor(name, shape, dt, kind="ExternalInput").ap()

    x = din("x", [S, D])
    cT = din("cT", [128, 8])
    ngT = din("ngT", [128, 8])
    badaT = din("badaT", [128, 24])
    wada = din("wada", [6, 128, 4096])
    wall = din("wall", [96, 128, 1024])
    wout = din("wout", [128, 8 * 1024])
    convw = din("convw", [128, 24])
    c31d = din("c31", [1, 8])
    btd = din("bt", [8, 128, 1024])
    fgd = din("fg", [1, 1024])
    out = nc.dram_tensor("out", [S, D], F32, kind="ExternalOutput").ap()
    aGd = nc.dram_tensor("aGd", [8, 128, S], BF16, kind="Internal").ap()
    cGd = nc.dram_tensor("cGd", [8, 128, S], BF16, kind="Internal").ap()
    dbg = None
    if stop_after is not None:
        dbg = nc.dram_tensor("dbg", [128, KC * S], BF16, kind="ExternalOutput").ap()

    st = ExitStack()
    with st:
        def sb(name, shape, dt):
            return st.enter_context(nc.sbuf_tensor(name, shape, dt))

        hT = sb("hT", [128, KC * S], BF16)
        hT3 = hT[:].rearrange("p (k s) -> p k s", k=KC)
        wring = sb("wring", [128, 8 * 1024], BF16)
        identf = sb("identf", [128, 128], F32)
        onesf = sb("onesf", [128, 128], F32)
        ident = sb("ident", [128, 128], BF16)
        onesb = sb("onesb", [128, 128], BF16)
        Emat = sb("Emat", [128, 16 * 128], BF16)
        Emat3 = Emat[:].rearrange("p (j c) -> p j c", j=16)
        pmask = sb("pmask", [128, NT * 16], F32)
        top8 = sb("top8", [128, NT * 8], F32)
        top83 = top8[:].rearrange("p (q e) -> p q e", e=8)
        fgb = sb("fgb", [128, 1024], F32)
        small = sb("small", [128, 256], F32)
        stat = sb("stat", [128, 4 * NT], F32)
        ARW = 26 * 1024
        arena_t = sb("arena", [128, ARW], F32)
        psum = st.enter_context(nc.psum_tensor("psum", [128, 4096], F32))
        psum_bf = psum[:].bitcast(BF16)
        AR = Arena(arena_t[:], ARW)
        P = Prog(nc)

        def finish_debug(src_ap):
            P.barrier()
            P.dma("sp", lambda e: e.dma_start(out=dbg[:, 0:src_ap.shape[1]], in_=src_ap), "dbg", writes=["dbg"])
            P.barrier()
            P.emit(st)

        def bank(i, a=0, b=512):
            return psum[:, i * 512 + a:i * 512 + b]

        def bk(i):
            return ("bank", i)

        cTs = small[:, 0:8]
        ngTs = small[:, 8:16]
        badas = small[:, 16:40]
        scs = small[:, 40:48]
        modT = small[:, 48:72]
        shiftT = small[:, 48:56]
        scaleT = small[:, 56:64]
        gateT = small[:, 64:72]
        g1T = small[:, 72:80]
        c31s = small[:, 80:88]
        cws = small[:, 88:112]
        kms = small[:, 112:128]
        epsc = small[:, 128:129]
        kmT = sb("kmT", [128, 16], BF16)

        def wr(s, i):
            return wring[:, (s * 4 + i) * 1024:(s * 4 + i + 1) * 1024]

        def load_w(s, i, cb):
            P.dma("pool", lambda e: e.dma_start(out=wr(s, i), in_=wall[cb]), f"wr{s}{i}", writes=[("wr", s, i)])

        for i, cb in enumerate((0, 8, 16, 24)):
            load_w(0, i, cb)
        P.dma("sp", lambda e: e.dma_start(out=cTs, in_=cT), "c0", writes=["cTs"])
        P.dma("sp", lambda e: e.dma_start(out=ngTs, in_=ngT), "c1", writes=["ngTs"])
        P.dma("sp", lambda e: e.dma_start(out=badas, in_=badaT), "c2", writes=["badas"])
        P.dma("sp", lambda e: e.dma_start(out=c31s, in_=c31d.to_broadcast([128, 8])), "c3", writes=["c31s"])
        P.dma("sp", lambda e: e.dma_start(out=cws, in_=convw), "c4", writes=["cws"])
        P.dma("sp", lambda e: e.dma_start(out=fgb[:], in_=fgd.to_broadcast([128, 1024])), "c5", writes=["fgb"])
        P.op("pool", lambda e: e.memset(identf[:], 1.0), writes=["identf"])
        P.op("pool", lambda e: e.affine_select(out=identf[:], in_=identf[:], pattern=[[-1, 128]],
                                               compare_op=ALU.is_equal, fill=0.0, base=0, channel_multiplier=1),
             reads=["identf"], writes=["identf"])
        P.op("pool", lambda e: e.memset(onesf[:], 1.0), writes=["onesf"])
        P.op("pool", lambda e: e.memset(onesb[:], 1.0), writes=["onesb"])
        P.op("pool", lambda e: e.memset(Emat[:], 1.0), writes=["Emat"])
        P.op("pool", lambda e: e.affine_select(out=Emat3, in_=Emat3, pattern=[[-1, 16], [0, 128]],
                                               compare_op=ALU.is_equal, fill=0.0, base=0, channel_multiplier=1),
             reads=["Emat"], writes=["Emat"])
        P.op("pool", lambda e: e.memset(pmask[:], -1e30), writes=["pmask"])
        pm4 = pmask[:].rearrange("p (b r j) -> p b r j", r=2, j=16)
        P.op("pool", lambda e: e.affine_select(out=pm4, in_=pm4, pattern=[[-1, NB], [0, 2], [1, 16]],
                                               compare_op=ALU.is_ge, fill=0.0, base=0, channel_multiplier=0),
             reads=["pmask"], writes=["pmask"])
        P.op("pool", lambda e: e.memset(top8[:], -1e29), writes=["top8"])
        P.op("pool", lambda e: e.memset(kmT[:], 0.0), writes=["kmT"])
        P.op("pool", lambda e: e.memset(epsc, EPS), writes=["epsc"])
        P.op("dve", lambda e: e.tensor_copy(out=ident[:], in_=identf[:]), reads=["identf"], writes=["ident"])

        AR.reset()
        QT = AR.alloc(S, BF16)
        KT = AR.alloc(S, BF16)
        GT = AR.alloc(S, BF16)
        VT = AR.alloc(S, BF16)
        V = AR.alloc(NT * 129 + 7, BF16)
        V3 = V[:, 0:NT * 129].rearrange("p (t d) -> p t d", d=129)
        at_ = [AR.alloc(128, F32) for _ in range(4)]
        rden = AR.alloc(8, F32)
        MnT = AR.alloc(S, BF16)
        scm = AR.alloc(NT * 16, F32)
        ltb = AR.alloc(NT * 16, F32)
        mneg = AR.alloc(NT * 16, BF16)
        mneg3 = mneg.rearrange("p (q j) -> p q j", j=16)
        BT = [AR.alloc(1024, F32) for _ in range(2)]
        NPT = 6
        Pt = [AR.alloc(512, BF16) for _ in range(NPT)]
        aGh = [AR.alloc(S, BF16) for _ in range(2)]

        rr = [0]

        def nextbank():
            b = rr[0] % 4
            rr[0] += 1
            return b

        def proj_chunk(which, tc, ws):
            w_ = wr(ws, (0, 1, 3, 2)[which])
            wkey = ("wr", ws, (0, 1, 3, 2)[which])
            b = nextbank()
            for kc in range(8):
                P.op("pe", lambda e, b=b, kc=kc, tc=tc, w_=w_: e.matmul(
                    bank(b), lhsT=w_[:, kc * 128:(kc + 1) * 128], rhs=hT3[:, kc, tc * 512:(tc + 1) * 512],
                    start=(kc == 0), stop=(kc == 7)),
                    reads=[wkey, ("hT", tc)], writes=[bk(b)], signal=(kc == 7))
            if which == 0:
                P.op("act", lambda e, b=b, tc=tc: e.activation(out=QT[:, tc * 512:(tc + 1) * 512], in_=bank(b), func=AF.Copy,
                                                               scale=float(HD ** -0.5)),
                     reads=[bk(b)], writes=[("QT", tc)])
            elif which == 1:
                for hb in range(2):
                    P.op("act", lambda e, b=b, tc=tc, hb=hb: e.activation(
                        out=KT[:, tc * 512 + hb * 256:tc * 512 + (hb + 1) * 256], in_=bank(b, hb * 256, (hb + 1) * 256),
                        func=AF.Copy, accum_out=kms[:, 2 * tc + hb:2 * tc + hb + 1]),
                        reads=[bk(b)], writes=[("KT", tc), ("kms", tc)])
            elif which == 2:
                P.op("act", lambda e, b=b, tc=tc: e.activation(out=GT[:, tc * 512:(tc + 1) * 512], in_=bank(b), func=AF.Silu),
                     reads=[bk(b)], writes=[("GT", tc)])
            else:
                P.op("act", lambda e, b=b, tc=tc: e.activation(out=VT[:, tc * 512:(tc + 1) * 512], in_=bank(b), func=AF.Copy),
                     reads=[bk(b)], writes=[("VT", tc)])

        P0_WORDS = 2 * 4096 + 4 * 1024 + 4 * 512 + 512 + 2 * 512
        assert AR.off <= ARW - P0_WORDS or True
        AR.off = ARW - P0_WORDS
        wa = [AR.alloc(4096, F32) for _ in range(2)]
        P.op("act", lambda e: e.activation(out=scs, in_=cTs, func=AF.Silu), reads=["cTs"], writes=["scs"])
        accm = [AR.alloc(512, F32) for _ in range(2)]
        for g in range(6):
            P.dma("sp", lambda e, g=g: e.dma_start(out=wa[g % 2], in_=wada[g]), f"wa{g % 2}", writes=[("wa", g % 2)])
            for kc in range(8):
                if kc == 0:
                    P.op("dve", lambda e, g=g: e.tensor_scalar(out=accm[g % 2], in0=wa[g % 2][:, 0:512], scalar1=scs[:, 0:1],
                                                               scalar2=None, op0=ALU.mult),
                         reads=[("wa", g % 2), "scs"], writes=[("accm", g % 2)])
                else:
                    P.op("dve", lambda e, g=g, kc=kc: e.scalar_tensor_tensor(
                        out=accm[g % 2], in0=wa[g % 2][:, kc * 512:(kc + 1) * 512], scalar=scs[:, kc:kc + 1], in1=accm[g % 2],
                        op0=ALU.mult, op1=ALU.add),
                        reads=[("wa", g % 2), "scs", ("accm", g % 2)], writes=[("accm", g % 2)])
            for j in range(4):
                col = g * 4 + j
                P.op("pe", lambda e, g=g, j=j, col=col: e.matmul(
                    bank(7, col, col + 1), lhsT=accm[g % 2][:, j * 128:(j + 1) * 128], rhs=onesf[:, 0:1], start=True, stop=True),
                    reads=[("accm", g % 2), "onesf"], writes=[bk(7)], signal=(j == 3))
        P.op("dve", lambda e: e.tensor_tensor(out=modT, in0=bank(7, 0, 24), in1=badas, op=ALU.add),
             reads=[bk(7), "badas"], writes=["modT"])
        P.op("dve", lambda e: e.scalar_tensor_tensor(out=g1T, in0=scaleT, scalar=1.0, in1=ngTs, op0=ALU.add, op1=ALU.mult),
             reads=["modT", "ngTs"], writes=["g1T"])

        if stop_after == "0a":
            P.op("dve", lambda e: e.tensor_copy(out=hT[:, 0:32], in_=small[:, 48:80]), reads=["modT", "g1T"], writes=["hTdbg"])
            finish_debug(hT[:, 0:32])
            return nc
        xt = [AR.alloc(1024, F32) for _ in range(4)]
        xn = [AR.alloc(1024, BF16) for _ in range(4)]
        junk = AR.alloc(1024, BF16)
        ss = stat[:, 0:NT]
        sq = stat[:, NT:2 * NT]
        rstd = stat[:, 2 * NT:3 * NT]

        def pT(s_, kc, a, b):
            base = (4 * s_ + kc // 2) * 1024 + (kc % 2) * 512
            return psum_bf[:, base + a:base + b]

        for tg in range(NC5):
            s_ = 1
            for i in range(4):
                tt = tg * 4 + i
                P.dma("sp", lambda e, tt=tt, i=i: e.dma_start(out=xt[i], in_=x[tt * 128:(tt + 1) * 128, :]),
                      f"xt{i}", writes=[("xt", i)])
                P.op("act", lambda e, tt=tt, i=i: e.activation(out=junk, in_=xt[i], func=AF.Square, accum_out=ss[:, tt:tt + 1]),
                     reads=[("xt", i)], writes=[("ss", tt), "junk"])
            P.op("act", lambda e, tg=tg: e.activation(out=sq[:, tg * 4:tg * 4 + 4], in_=ss[:, tg * 4:tg * 4 + 4], func=AF.Sqrt,
                                                      bias=epsc, scale=1.0 / D),
                 reads=[("ss", tg * 4 + i) for i in range(4)] + ["epsc"], writes=[("sq", tg)])
            P.op("dve", lambda e, tg=tg: e.reciprocal(out=rstd[:, tg * 4:tg * 4 + 4], in_=sq[:, tg * 4:tg * 4 + 4]),
                 reads=[("sq", tg)], writes=[("rstd", tg)])
            for i in range(4):
                tt = tg * 4 + i
                P.op("act", lambda e, tt=tt, i=i: e.activation(out=xn[i], in_=xt[i], func=AF.Copy, scale=rstd[:, tt:tt + 1]),
                     reads=[("xt", i), ("rstd", tg)], writes=[("xn", i)])
                for kc in range(8):
                    P.op("pe", lambda e, s_=s_, kc=kc, i=i: e.transpose(pT(s_, kc, i * 128, (i + 1) * 128),
                                                                         xn[i][:, kc * 128:(kc + 1) * 128], ident[:]),
                         reads=[("xn", i), "ident"], writes=[bk(4 * s_ + kc // 2)], signal=(kc == 7))
            for kc in range(8):
                P.op("dve", lambda e, s_=s_, kc=kc, tg=tg: e.tensor_scalar(
                    out=hT3[:, kc, tg * 512:(tg + 1) * 512], in0=pT(s_, kc, 0, 512),
                    scalar1=g1T[:, kc:kc + 1], scalar2=shiftT[:, kc:kc + 1], op0=ALU.mult, op1=ALU.add),
                    reads=[bk(4 * s_ + kc // 2), "g1T", "modT"], writes=[("hT", tg)])
            if tg >= 1:
                for which in range(4):
                    proj_chunk(which, tg - 1, 0)
        for which in range(4):
            proj_chunk(which, NC5 - 1, 0)
        all_hT = [("hT", tg) for tg in range(NC5)]

        P.op("pool", lambda e: e.memset(MnT, 0.0), reads=all_hT, writes=["MnT"])
        P.op("pool", lambda e: e.memset(V[:, 0:NT * 129], 1.0), reads=all_hT, writes=[("V", tg) for tg in range(NC5)])

        for h in range(H):
            ws = h % 2
            wq, wk, wv, wg = (wr(ws, i) for i in range(4))
            wkeys = [("wr", ws, i) for i in range(4)]
            ns = (h + 1) % 2
            if h + 1 < H:
                for i, cb in enumerate((h + 1, 8 + h + 1, 16 + h + 1, 24 + h + 1)):
                    load_w(ns, i, cb)
            else:
                for i, cb in enumerate((32, 40, 48, 56)):
                    load_w(ns, i, cb)
            P.dma("sp", lambda e, h=h: e.dma_start(out=BT[h % 2], in_=btd[h]), f"bt{h % 2}",
                  reads=(all_hT if h == 0 else []), writes=[("BT", h % 2)])
            def proj_fm(which):
                if h == 0:
                    return
                for tc in range(NC5):
                    proj_chunk(which, tc, ws)

            proj_fm(0)
            proj_fm(1)
            if h > 0:
                proj_chunk(2, 0, ws)
            P.op("dve", lambda e: e.tensor_scalar(out=kmT[:, 0:NB], in0=kms[:, 0:NB], scalar1=1.0 / 256, scalar2=None, op0=ALU.mult),
                 reads=[("kms", tc) for tc in range(NC5)] + ["kmT"], writes=["kmT"])
            for qt in range(NT):
                P.op("pe", lambda e, qt=qt: e.matmul(bank(7, qt * 16, (qt + 1) * 16), lhsT=QT[:, qt * 128:(qt + 1) * 128],
                                                     rhs=kmT[:, 0:16], start=True, stop=True),
                     reads=[("QT", qt // 4), "kmT"], writes=[bk(7)], signal=(qt == NT - 1))
            P.op("dve", lambda e: e.tensor_tensor(out=scm, in0=bank(7, 0, NT * 16), in1=pmask[:], op=ALU.add),
                 reads=[bk(7), "pmask"], writes=["scm"])
            for qt in range(8, NT):
                P.op("dve", lambda e, qt=qt: e.max(out=top83[:, qt, :], in_=scm[:, qt * 16:(qt + 1) * 16]),
                     reads=["scm"], writes=[("top8", qt)])
            scm3 = scm.rearrange("p (q j) -> p q j", j=16)
            ltb3 = ltb.rearrange("p (q j) -> p q j", j=16)
            P.op("dve", lambda e: e.tensor_tensor(out=ltb3, in0=scm3, in1=top83[:, :, 2:3].to_broadcast([128, NT, 16]), op=ALU.is_lt),
                 reads=["scm", "top8"] + [("top8", qt) for qt in range(8, NT)], writes=["ltb"])
            P.op("dve", lambda e: e.tensor_scalar(out=mneg, in0=ltb, scalar1=NEG, scalar2=None, op0=ALU.mult),
                 reads=["ltb"], writes=["mneg"])
            if h > 0:
                for tc in range(1, NC5):
                    proj_chunk(2, tc, ws)
            proj_fm(3)
            nbk = S // 1024
            for qt in range(NT):
                P.op("pe", lambda e, qt=qt: e.transpose(psum_bf[0:16, qt * 128:(qt + 1) * 128], mneg3[:, qt, :], ident[:]),
                     reads=["mneg", "ident"], writes=[bk(qt // 8)], signal=(qt % 8 == 7))
            for b in range(nbk):
                P.op("dve", lambda e, b=b: e.tensor_copy(out=MnT[0:16, b * 1024:(b + 1) * 1024],
                                                         in_=psum_bf[0:16, b * 1024:(b + 1) * 1024]),
                     reads=[bk(b)], writes=["MnT"])
            for tg in range(NC5):
                b = nextbank()
                for i in range(4):
                    tt = tg * 4 + i
                    P.op("pe", lambda e, b=b, i=i, tt=tt: e.transpose(psum_bf[:, b * 1024 + i * 128:b * 1024 + (i + 1) * 128],
                                                                      VT[:, tt * 128:(tt + 1) * 128], ident[:]),
                         reads=[("VT", tg), "ident"], writes=[bk(b)], signal=(i == 3))
                P.op("dve", lambda e, b=b, tg=tg: e.tensor_copy(
                    out=V3[:, tg * 4:(tg + 1) * 4, 0:128],
                    in_=psum_bf[:, b * 1024:b * 1024 + 512].rearrange("p (t d) -> p t d", d=128)),
                     reads=[bk(b)], writes=[("V", tg)])
            if stop_after == "A1":
                finish_debug(arena_t[:, 0:2 * S].bitcast(BF16))
                return nc
            if stop_after == "A2":
                finish_debug(MnT)
                return nc
            steps = [(qb, jb) for qb in range(NB) for jb in range(qb + 1)]
            LAG = 3
            bts = BT[h % 2]

            def qk(i):
                qb, jb = steps[i]
                sbk = i - (i // 4) * 4
                for kt in range(2):
                    P.op("pe", lambda e, kt=kt, qb=qb, jb=jb, sbk=sbk: e.matmul(
                        bank(sbk, kt * 256, (kt + 1) * 256), lhsT=KT[:, jb * 256 + kt * 128:jb * 256 + (kt + 1) * 128],
                        rhs=QT[:, qb * 256:(qb + 1) * 256], start=True, stop=(jb == qb or qb <= 3)),
                        reads=[("KT", jb // 2), ("QT", qb // 2)], writes=[bk(sbk)], signal=((jb == qb or qb <= 3) and kt == 1))
                    if jb < qb and qb > 3:
                        P.op("pe", lambda e, kt=kt, qb=qb, jb=jb, sbk=sbk: e.matmul(
                            bank(sbk, kt * 256, (kt + 1) * 256), lhsT=Emat3[:, jb, :],
                            rhs=MnT[:, qb * 256:(qb + 1) * 256], start=False, stop=True),
                            reads=["Emat", "MnT"], writes=[bk(sbk)], signal=(kt == 1))
                if jb >= qb - 1:
                    off = 0 if jb == qb else 512
                    P.op("dve", lambda e, sbk=sbk, off=off, bts=bts: e.tensor_tensor(out=bank(sbk), in0=bank(sbk), in1=bts[:, off:off + 512],
                                                                            op=ALU.add),
                         reads=[bk(sbk), ("BT", h % 2)], writes=[bk(sbk)])
                    P.op("act", lambda e, sbk=sbk, i=i: e.activation(out=Pt[i % NPT], in_=bank(sbk), func=AF.Exp),
                         reads=[bk(sbk)], writes=[("Pt", i % NPT)])
                else:
                    P.op("act", lambda e, sbk=sbk, i=i, h=h: e.activation(out=Pt[i % NPT], in_=bank(sbk), func=AF.Exp,
                                                                     bias=c31s[:, h:h + 1]),
                         reads=[bk(sbk), "c31s"], writes=[("Pt", i % NPT)])

                a2 = qb % 2
                if i in pool_steps[qb]:
                    if i == pool_steps[qb][0]:
                        P.op("pool", lambda e, i=i, a2=a2: e.tensor_copy(out=accP[a2], in_=Pt[i % NPT]),
                             reads=[("Pt", i % NPT)], writes=[("acc", a2)])
                    else:
                        P.op("pool", lambda e, i=i, a2=a2: e.tensor_tensor(out=accP[a2], in0=accP[a2], in1=Pt[i % NPT], op=ALU.add),
                             reads=[("Pt", i % NPT), ("acc", a2)], writes=[("acc", a2)])
                    if i == pool_steps[qb][-1]:
                        P.op("pool", lambda e, a2=a2: e.tensor_tensor(out=accF[a2], in0=accP[a2][:, 0:256], in1=accP[a2][:, 256:512],
                                                                      op=ALU.add),
                             reads=[("acc", a2)], writes=[("accF", a2)])
                        P.op("pool", lambda e, a2=a2: e.tensor_copy(out=dhi[a2], in_=accF[a2]),
                             reads=[("accF", a2)], writes=[("dhi", a2)])
                        P.op("pool", lambda e, a2=a2: e.tensor_tensor(out=dlo[a2], in0=accF[a2], in1=dhi[a2], op=ALU.subtract),
                             reads=[("accF", a2), ("dhi", a2)], writes=[("dlo", a2)])

            def obank(qb, qt2):
                return 4 + (2 * qb + qt2) % 3

            def ocol(qt2):
                return 0

            def pv(i):
                qb, jb = steps[i]
                for qt2 in range(2):
                    ob = obank(qb, qt2)
                    for kt in range(2):
                        if jb == qb and kt == 1 and qt2 == 0:
                            continue
                        first = (jb == 0 and kt == 0)
                        last = (jb == qb and kt == 1) or (jb == qb and kt == 0 and qt2 == 0)
                        oc = ocol(qt2)
                        P.op("pe", lambda e, kt=kt, jb=jb, i=i, ob=ob, qt2=qt2, first=first, last=last, oc=oc: e.matmul(
                            bank(ob, oc, oc + 129), lhsT=Pt[i % NPT][:, kt * 256 + qt2 * 128:kt * 256 + (qt2 + 1) * 128],
                            rhs=V3[:, jb * 2 + kt, :], start=first, stop=last),
                            reads=[("V", jb // 2), ("Pt", i % NPT)], writes=[bk(ob)], signal=(qt2 == 1 and kt == 1))

            def finalize_a(qb):
                for qt2 in range(2):
                    ob = obank(qb, qt2)
                    a_ = at_[2 * (qb % 2) + qt2]
                    rc = rden[:, 2 * (qb % 2) + qt2:2 * (qb % 2) + qt2 + 1]
                    akey = ("at", 2 * (qb % 2) + qt2)
                    oc = ocol(qt2)
                    P.op("dve", lambda e, ob=ob, rc=rc, oc=oc: e.reciprocal(out=rc, in_=bank(ob, oc + 128, oc + 129)),
                         reads=[bk(ob)], writes=[("rden", qb % 2, qt2)])
                    P.op("dve", lambda e, ob=ob, rc=rc, a_=a_, oc=oc: e.tensor_scalar(out=a_, in0=bank(ob, oc, oc + 128), scalar1=rc, scalar2=None,
                                                                             op0=ALU.mult),
                         reads=[bk(ob), ("rden", qb % 2, qt2)], writes=[akey])

            def finalize(qb):
                tb0 = (qb % 2) * 256
                for qt2 in range(2):
                    a_ = at_[2 * (qb % 2) + qt2]
                    akey = ("at", 2 * (qb % 2) + qt2)
                    P.op("pe", lambda e, a_=a_, qt2=qt2, tb0=tb0: e.transpose(bank(7, tb0 + qt2 * 128, tb0 + (qt2 + 1) * 128), a_, identf[:]),
                         reads=[akey, "identf"], writes=[bk(7)], signal=(qt2 == 1))
                P.op("dve", lambda e, qb=qb, tb0=tb0, h=h: e.tensor_tensor(out=aGh[h % 2][:, qb * 256:(qb + 1) * 256],
                                                                         in0=bank(7, tb0, tb0 + 256),
                                                                         in1=GT[:, qb * 256:(qb + 1) * 256], op=ALU.mult),
                     reads=[bk(7), ("GT", qb // 2)], writes=[("aGh", h % 2)])

            n = len(steps)
            pool_steps = {qb: [] for qb in range(NB)}
            pe_steps = {qb: [] for qb in range(NB)}
            for i_, (qb_, jb_) in enumerate(steps):
                (pool_steps if (POOL_DEN and i_ % POOL_DEN == POOL_DEN - 1) else pe_steps)[qb_].append(i_)
            pend = []
            for i in range(n + LAG):
                if i < n:
                    qk(i)
                if i - LAG >= 0:
                    pv(i - LAG)
                    while pend and pend[0][1] <= i:
                        finalize(pend.pop(0)[0])
                    if steps[i - LAG][1] == steps[i - LAG][0]:
                        finalize_a(steps[i - LAG][0])
                        pend.append((steps[i - LAG][0], i + 3))
            for qb_, _ in pend:
                finalize(qb_)
            P.dma("sp", lambda e, h=h: e.dma_start(out=aGd[h], in_=aGh[h % 2]), f"aGd{h % 2}",
                  reads=[("aGh", h % 2)], writes=[("aGd", h)])
            if stop_after == "A3":
                finish_debug(aGh[0])
                return nc
        P.barrier()

        AR.reset()
        PB_WORDS = ((S + 2 + 7) // 8 * 8) + 8 * 512 + 2 * (S // 2)
        AR.off = (ARW - PB_WORDS) // 8 * 8
        N_PRE = min(32, AR.off // 512)
        order = []
        for j in range(8):
            order += [(j, 80 + j), (8 + j, 88 + j), (16 + j, 64 + j), (24 + j, 72 + j)]
        Wslot = {wi: arena_t[:, k * 512:(k + 1) * 512].bitcast(BF16) for k, (wi, cb) in enumerate(order)}

        def load_wj(k):
            wi, cb = order[k]
            P.dma("pool", lambda e, wi=wi, cb=cb: e.dma_start(out=Wslot[wi], in_=wall[cb]), f"wj{wi}", writes=[("Wj", wi)])

        u = AR.alloc(S + 2, F32)
        cxs = [AR.alloc(512, F32) for _ in range(2)]
        ybuf = [AR.alloc(512, F32) for _ in range(2)]
        sgb = [AR.alloc(512, F32) for _ in range(2)]
        zb = [AR.alloc(512, F32) for _ in range(2)]
        cGs = [AR.alloc(S, BF16) for _ in range(2)]
        P.op("pool", lambda e: e.memset(u[:, 0:2], 0.0), writes=[("u", -1)])
        for c in range(8):
            ws = c % 2
            wcb, wcc, wcx, wgc = (wr(ws, i) for i in range(4))
            wkeys = [("wr", ws, i) for i in range(4)]
            if c + 1 < 8:
                for i, cb in enumerate((32 + c + 1, 40 + c + 1, 48 + c + 1, 56 + c + 1)):
                    load_w((c + 1) % 2, i, cb)
            if c == 1:
                for k in range(N_PRE):
                    load_wj(k)
            for tc in range(NC5):
                for wi, w_ in enumerate((wcb, wcc, wcx, wgc)):
                    for kc in range(8):
                        P.op("pe", lambda e, wi=wi, w_=w_, kc=kc, tc=tc: e.matmul(
                            bank(wi + 4 * (tc % 2)), lhsT=w_[:, kc * 128:(kc + 1) * 128], rhs=hT3[:, kc, tc * 512:(tc + 1) * 512],
                            start=(kc == 0), stop=(kc == 7)),
                            reads=[wkeys[wi], "hT"], writes=[bk(wi + 4 * (tc % 2))], signal=(kc == 7))
                o = 4 * (tc % 2)
                k2 = tc % 2
                P.op("act", lambda e, o=o, k2=k2: e.activation(out=cxs[k2], in_=bank(o + 2), func=AF.Copy),
                     reads=[bk(o + 2)], writes=[("cxs", k2)])
                P.op("dve", lambda e, o=o, k2=k2, tc=tc: e.tensor_tensor(out=u[:, 2 + tc * 512:2 + (tc + 1) * 512], in0=bank(o + 1),
                                                                         in1=cxs[k2], op=ALU.mult),
                     reads=[bk(o + 1), ("cxs", k2)], writes=[("u", tc)])
                P.op("dve", lambda e, k2=k2, tc=tc, c=c: e.tensor_scalar(out=ybuf[k2], in0=u[:, 2 + tc * 512:2 + (tc + 1) * 512],
                                                                         scalar1=cws[:, c * 3 + 2:c * 3 + 3], scalar2=None, op0=ALU.mult),
                     reads=[("u", tc), "cws"], writes=[("y", k2)])
                P.op("dve", lambda e, k2=k2, tc=tc, c=c: e.scalar_tensor_tensor(
                    out=ybuf[k2], in0=u[:, 1 + tc * 512:1 + (tc + 1) * 512], scalar=cws[:, c * 3 + 1:c * 3 + 2], in1=ybuf[k2],
                    op0=ALU.mult, op1=ALU.add),
                    reads=[("u", tc), ("u", tc - 1), ("y", k2)], writes=[("y", k2)])
                P.op("dve", lambda e, k2=k2, tc=tc, c=c: e.scalar_tensor_tensor(
                    out=ybuf[k2], in0=u[:, tc * 512:(tc + 1) * 512], scalar=cws[:, c * 3:c * 3 + 1], in1=ybuf[k2],
                    op0=ALU.mult, op1=ALU.add),
                    reads=[("u", tc), ("u", tc - 1), ("y", k2)], writes=[("y", k2)])
                P.op("act", lambda e, o=o, k2=k2: e.activation(out=sgb[k2], in_=bank(o + 3), func=AF.Silu),
                     reads=[bk(o + 3)], writes=[("sg", k2)])
                P.op("dve", lambda e, o=o, k2=k2: e.tensor_tensor(out=zb[k2], in0=bank(o), in1=ybuf[k2], op=ALU.mult),
                     reads=[bk(o), ("y", k2)], writes=[("z", k2)])
                P.op("dve", lambda e, k2=k2, tc=tc, c=c: e.tensor_tensor(out=cGs[c % 2][:, tc * 512:(tc + 1) * 512], in0=zb[k2],
                                                                         in1=sgb[k2], op=ALU.mult),
                     reads=[("z", k2), ("sg", k2)], writes=[("cGs", c % 2)])
            P.dma("sp", lambda e, c=c: e.dma_start(out=cGd[c], in_=cGs[c % 2]), f"cGd{c % 2}",
                  reads=[("cGs", c % 2)], writes=[("cGd", c)])
        P.barrier()

        AR.reset()
        AR.off = 32 * 512
        Wj = [Wslot[wi] for wi in range(32)]
        aGc = [AR.alloc(8 * 256, BF16) for _ in range(2)]
        cGc = [AR.alloc(8 * 256, BF16) for _ in range(2)]
        sgm = [AR.alloc(512, F32) for _ in range(2)]
        tmp = [AR.alloc(512, F32) for _ in range(2)]
        mst = [AR.alloc(8 * 256, BF16) for _ in range(2)]
        for k in range(N_PRE, 32):
            load_wj(k)
        Wg3 = wring[:].rearrange("p (k n) -> p k n", k=8)
        gbc = AR.alloc(1024, F32)
        wst1 = AR.alloc(1024, F32)
        for kc in range(8):
            P.op("dve", lambda e, kc=kc: e.tensor_scalar(out=wst1[:, kc * 128:(kc + 1) * 128], in0=identf[:], scalar1=gateT[:, kc:kc + 1],
                                                         scalar2=None, op0=ALU.mult),
                 writes=[("Dk", kc)])
            P.op("pe", lambda e, kc=kc: e.matmul(bank(4 + kc // 4, (kc % 4) * 128, (kc % 4 + 1) * 128), lhsT=onesf[:],
                                                 rhs=wst1[:, kc * 128:(kc + 1) * 128], start=True, stop=True),
                 reads=[("Dk", kc)], writes=[bk(4 + kc // 4)], signal=(kc % 4 == 3))
        P.op("dve", lambda e: e.tensor_copy(out=gbc, in_=psum[:, 2048:3072]), reads=[bk(4), bk(5)], writes=["gbc"])
        wg_done = [0]

        def wg_piece(kc):
            P.dma("sp", lambda e, kc=kc: e.dma_start(out=wst1, in_=wout[:, kc * 1024:(kc + 1) * 1024]), "wst1",
                  writes=["wst1"] + [("Dk", k_) for k_ in range(8)])
            P.op("dve", lambda e, kc=kc: e.tensor_tensor(out=Wg3[:, kc, :], in0=wst1, in1=gbc, op=ALU.mult),
                 reads=["wst1", "gbc"], writes=["Wg"])
        aGd_v = aGd.rearrange("h p s -> p h s")
        cGd_v = cGd.rearrange("h p s -> p h s")
        for t in range(NCH):
            k2 = t % 2
            a3 = aGc[k2].rearrange("p (h s) -> p h s", h=8)
            c3 = cGc[k2].rearrange("p (h s) -> p h s", h=8)
            m3 = mst[k2].rearrange("p (h s) -> p h s", h=8)
            P.dma("sp", lambda e, t=t, a3=a3: e.dma_start(out=a3, in_=aGd_v[:, :, t * 256:(t + 1) * 256]), f"aGc{k2}",
                  writes=[("aGc", k2)])
            P.dma("sp", lambda e, t=t, c3=c3: e.dma_start(out=c3, in_=cGd_v[:, :, t * 256:(t + 1) * 256]), f"cGc{k2}",
                  writes=[("cGc", k2)])
            if 2 <= t < 10:
                wg_piece(t - 2)
                wg_done[0] = t - 1
            for j in range(8):
                idx = t * 8 + j
                bA, bB = 2 * (idx % 2), 2 * (idx % 2) + 1
                groups = [(bA, 0, Wj[j], ("Wj", j), a3, ("aGc", k2)),
                          (bA, 256, Wj[8 + j], ("Wj", 8 + j), c3, ("cGc", k2)),
                          (bB, 0, Wj[16 + j], ("Wj", 16 + j), None, None),
                          (bB, 256, Wj[24 + j], ("Wj", 24 + j), None, None)]
                for gi, (bb, off, w_, wkey, src, skey) in enumerate(groups):
                    for kc in range(8):
                        if src is None:
                            rhs = hT3[:, kc, t * 256:(t + 1) * 256]
                            rk = ("hT", t)
                        else:
                            rhs = src[:, kc, :]
                            rk = skey
                        P.op("pe", lambda e, bb=bb, off=off, w_=w_, kc=kc, rhs=rhs: e.matmul(
                            bank(bb, off, off + 256), lhsT=w_[:, kc * 128:(kc + 1) * 128], rhs=rhs,
                            start=(kc == 0), stop=(kc == 7)),
                            reads=[wkey, rk], writes=[bk(bb)], signal=(kc == 7 and gi % 2 == 1))
                i2 = idx % 2
                P.op("act", lambda e, bB=bB, i2=i2: e.activation(out=sgm[i2], in_=bank(bB), func=AF.Sigmoid),
                     reads=[bk(bB)], writes=[("sgm", i2)])
                P.op("dve", lambda e, bA=bA, i2=i2: e.tensor_tensor(out=tmp[i2], in0=bank(bA), in1=sgm[i2], op=ALU.mult),
                     reads=[bk(bA), ("sgm", i2)], writes=[("tmp", i2)])
                P.op("dve", lambda e, i2=i2, j=j, m3=m3: e.tensor_tensor(out=m3[:, j, :], in0=tmp[i2][:, 0:256], in1=tmp[i2][:, 256:512],
                                                                         op=ALU.add),
                     reads=[("tmp", i2)], writes=[("mst", k2)])
            P.op("pool", lambda e, t=t, m3=m3: e.tensor_copy(out=hT3[:, :, t * 256:(t + 1) * 256], in_=m3),
                 reads=[("mst", k2)], writes=[("hT", t)])
        for kc in range(wg_done[0], 8):
            wg_piece(kc)
        P.barrier()

        AR.reset()
        xt2 = [AR.alloc(1024, F32) for _ in range(4)]
        rb = [AR.alloc(1024, F32) for _ in range(3)]
        ot = [AR.alloc(1024, F32) for _ in range(4)]
        junk2 = AR.alloc(1024, BF16)
        ss2 = stat[:, 0:NT]
        sq2 = stat[:, NT:2 * NT]
        rstd2 = stat[:, 2 * NT:3 * NT]
        out_keys = []
        def load_x2(tt):
            P.dma("sp", lambda e, tt=tt: e.dma_start(out=xt2[tt % 4], in_=x[tt * 128:(tt + 1) * 128, :]), f"xt2{tt % 4}",
                  writes=[("xt2", tt % 4)])

        PF = 3
        for tt in range(min(PF, NT)):
            load_x2(tt)
        for tt in range(NT):
            k4 = tt % 4
            k3 = tt % 3
            if tt + PF < NT:
                load_x2(tt + PF)
            for half in range(2):
                bb = 2 * k4 + half
                for kc in range(8):
                    P.op("pe", lambda e, bb=bb, kc=kc, tt=tt, half=half: e.matmul(
                        bank(bb), lhsT=hT3[:, kc, tt * 128:(tt + 1) * 128], rhs=Wg3[:, kc, half * 512:(half + 1) * 512],
                        start=(kc == 0), stop=(kc == 7)),
                        reads=[], writes=[bk(bb)], signal=(kc == 7 and half == 1))
            P.op("dve", lambda e, k4=k4, k3=k3: e.tensor_tensor(out=rb[k3], in0=psum[:, (2 * k4) * 512:(2 * k4) * 512 + 1024],
                                                         in1=xt2[k4], op=ALU.add),
                 reads=[bk(2 * k4), bk(2 * k4 + 1), ("xt2", k4)], writes=[("rb", k3)])
            P.op("act", lambda e, k3=k3, tt=tt: e.activation(out=junk2, in_=rb[k3], func=AF.Square, accum_out=ss2[:, tt:tt + 1]),
                 reads=[("rb", k3)], writes=[("ss2", tt), "junk2"])
            P.op("act", lambda e, tt=tt: e.activation(out=sq2[:, tt:tt + 1], in_=ss2[:, tt:tt + 1], func=AF.Sqrt, bias=epsc,
                                                      scale=1.0 / D),
                 reads=[("ss2", tt), "epsc"], writes=[("sq2", tt)])
            P.op("dve", lambda e, tt=tt: e.reciprocal(out=rstd2[:, tt:tt + 1], in_=sq2[:, tt:tt + 1]),
                 reads=[("sq2", tt)], writes=[("rstd2", tt)])
            P.op("dve", lambda e, k3=k3, k4=k4, tt=tt: e.scalar_tensor_tensor(out=ot[k4], in0=rb[k3], scalar=rstd2[:, tt:tt + 1], in1=fgb[:],
                                                                       op0=ALU.mult, op1=ALU.mult),
                 reads=[("rb", k3), ("rstd2", tt), "fgb"], writes=[("ot", k4)])
            P.dma("sp", lambda e, tt=tt, k4=k4: e.dma_start(out=out[tt * 128:(tt + 1) * 128, :], in_=ot[k4]), f"out{k4}",
                  reads=[("ot", k4)], writes=[("out", tt)])
            out_keys.append(("out", tt))
        P.barrier()
        P.emit(st)
    return nc


def _bucket_table(n):
    d = np.arange(n)
    nf = np.maximum(d, 1).astype(np.float32)
    large = 16 + (np.log(nf / np.float32(16)) / np.float32(np.log(128 / 16)) * np.float32(16)).astype(np.int32)
    large = np.minimum(large, 31)
    return np.where(d < 16, d, large)


def prep_shared(norm_g, w_ada, b_ada, w_in, conv_w, w_o_attn, w_o_conv, w_out, rel_bias, final_g):
    f = np.float32
    sh = {}
    sh["ngT"] = np.ascontiguousarray(norm_g[0].reshape(8, 128).T).astype(f)
    sh["badaT"] = np.ascontiguousarray(b_ada[0].reshape(24, 128).T).astype(f)
    sh["wada"] = np.ascontiguousarray(w_ada[0].reshape(8, 128, 6, 512).transpose(2, 1, 0, 3)).reshape(6, 128, 4096)

    def colblocks(w):
        ncb = w.shape[1] // 128
        return np.ascontiguousarray(w.reshape(8, 128, ncb, 128).transpose(2, 1, 0, 3)).reshape(ncb, 128, 1024)

    sh["wall"] = np.concatenate([colblocks(w_in[0]), colblocks(w_o_attn[0]), colblocks(w_o_conv[0])], axis=0)
    sh["wout"] = np.ascontiguousarray(w_out[0].reshape(8, 128, 1024).transpose(1, 0, 2)).reshape(128, 8192)
    sh["convw"] = np.ascontiguousarray(conv_w[0].T.reshape(8, 128, 3).transpose(1, 0, 2)).reshape(128, 24)
    sh["c31"] = np.ascontiguousarray(rel_bias[31:32, :]).astype(f)
    sh["fg"] = np.ascontiguousarray(final_g.reshape(1, 1024)).astype(f)
    bkt = _bucket_table(512)
    k = np.arange(256)[:, None]
    q = np.arange(256)[None, :]
    d_own = q - k
    d_prev = q + 256 - k
    bt = np.empty((8, 128, 1024), f)
    for h in range(8):
        tb = rel_bias[:, h]
        own = np.where(d_own >= 0, tb[bkt[np.maximum(d_own, 0)]], f(NEG)).astype(f)
        prev = tb[bkt[d_prev]].astype(f)
        bt[h, :, 0:512] = own.reshape(2, 128, 256).transpose(1, 0, 2).reshape(128, 512)
        bt[h, :, 512:1024] = prev.reshape(2, 128, 256).transpose(1, 0, 2).reshape(128, 512)
    sh["bt"] = bt
    return sh


_CACHE = {}


def _get_prog(S):
    if S not in _CACHE:
        _CACHE[S] = build_program(S)
    return _CACHE[S]


def kernel(x, c, norm_g, w_ada, b_ada, w_in, conv_w, w_o_attn, w_o_conv, w_out, rel_bias, final_g):
    x = np.asarray(x, np.float32)
    c = np.asarray(c, np.float32)
    B, S, _ = x.shape
    sh = prep_shared(*(np.asarray(a, np.float32) for a in
                       (norm_g, w_ada, b_ada, w_in, conv_w, w_o_attn, w_o_conv, w_out, rel_bias, final_g)))
    nc = _get_prog(S)
    in_maps = []
    for b in range(B):
        m = dict(sh)
        m["x"] = np.ascontiguousarray(x[b])
        m["cT"] = np.ascontiguousarray(c[b].reshape(8, 128).T)
        in_maps.append(m)
    res = run_bass_kernel_spmd(nc, in_maps, core_ids=list(range(B)))
    return np.stack([np.asarray(r["out"], np.float32) for r in res.results], axis=0)
```

```python
import bisect
import numpy as np
from contextlib import ExitStack
import concourse.bass as bass
import concourse.mybir as mybir
from concourse.bass_utils import run_bass_kernel_spmd

F32 = mybir.dt.float32
BF16 = mybir.dt.bfloat16
AF = mybir.ActivationFunctionType
ALU = mybir.AluOpType
AX = mybir.AxisListType

D = 1024
KC = 8
H = 8
HD = 128
NEG = -30000.0
EPS = 1e-6
COMPUTE = ("pe", "act", "dve", "pool")
STRICT = ("act", "dve", "pool")
POOL_DEN = 0


class Prog:
    def __init__(self, nc):
        self.nc = nc
        self.ops = {e: [] for e in ("pe", "act", "dve", "pool", "sp")}
        self.sigseqs = {e: [] for e in COMPUTE}
        self.waited = {e: {} for e in self.ops}
        self.state = {}
        self.dma_cnt = {}

    def _resolve(self, dep):
        if dep[0] == "dma":
            return (("dma", dep[1]), dep[2])
        _, eng, seq = dep
        lst = self.sigseqs[eng]
        i = bisect.bisect_left(lst, seq)
        if i == len(lst):
            last = len(self.ops[eng]) - 1
            while self.ops[eng][last]["fn"] is None or self.ops[eng][last]["dma"] is not None:
                last -= 1
            assert last >= seq, (eng, seq, last)
            rec = self.ops[eng][last]
            assert not rec["signal"]
            rec["signal"] = True
            lst.append(last)
            i = len(lst) - 1
        return (("eng", eng), i + 1)

    def _collect(self, eng, reads, writes):
        deps = []
        for k in reads:
            st = self.state.get(k)
            if st and st[0] is not None:
                deps.append(st[0])
        for k in writes:
            st = self.state.get(k)
            if st:
                if st[0] is not None:
                    deps.append(st[0])
                deps.extend(st[1].values())
        waits = {}
        for d in deps:
            if d[0] == "eng" and d[1] == eng and eng not in STRICT:
                continue
            semkey, val = self._resolve(d)
            if self.waited[eng].get(semkey, 0) >= val:
                continue
            if waits.get(semkey, 0) < val:
                waits[semkey] = val
        for semkey, val in waits.items():
            self.waited[eng][semkey] = val
        return list(waits.items())

    def _update(self, me, reads, writes):
        for k in reads:
            st = self.state.setdefault(k, [None, {}])
            st[1][(me[0], me[1])] = me
        for k in writes:
            self.state[k] = [me, {}]

    def op(self, eng, fn, reads=(), writes=(), signal=True):
        waits = self._collect(eng, reads, writes)
        seq = len(self.ops[eng])
        self.ops[eng].append({"fn": fn, "waits": waits, "signal": signal, "dma": None})
        if signal:
            self.sigseqs[eng].append(seq)
        self._update(("eng", eng, seq), reads, writes)

    def dma(self, queue, fn, sem, reads=(), writes=(), inc=16):
        waits = self._collect(queue, reads, writes)
        val = self.dma_cnt.get(sem, 0) + inc
        self.dma_cnt[sem] = val
        self.ops[queue].append({"fn": fn, "waits": waits, "signal": False, "dma": (sem, inc)})
        self._update(("dma", sem, val), reads, writes)

    def barrier(self):
        targets = []
        for e in COMPUTE:
            n = len(self.ops[e])
            last = n - 1
            while last >= 0 and (self.ops[e][last]["fn"] is None or self.ops[e][last]["dma"] is not None):
                last -= 1
            if last >= 0:
                targets.append(self._resolve(("eng", e, last)))
        for name, val in self.dma_cnt.items():
            targets.append((("dma", name), val))
        for e in self.ops:
            waits = []
            for semkey, val in targets:
                if semkey == ("eng", e):
                    continue
                if self.waited[e].get(semkey, 0) >= val:
                    continue
                self.waited[e][semkey] = val
                waits.append((semkey, val))
            if waits:
                self.ops[e].append({"fn": None, "waits": waits, "signal": False, "dma": None})
        self.state = {}

    def emit(self, stack):
        nc = self.nc
        sems = {}
        for e in COMPUTE:
            sems[("eng", e)] = stack.enter_context(nc.semaphore(f"s_{e}"))
        for name in self.dma_cnt:
            sems[("dma", name)] = stack.enter_context(nc.semaphore(f"d_{name}"))
        block = stack.enter_context(nc.Block())
        engmap = {"pe": "tensor", "act": "scalar", "dve": "vector", "pool": "gpsimd", "sp": "sync"}

        def make(ename):
            recs = self.ops[ename]

            def body(eng):
                for rec in recs:
                    for semkey, val in rec["waits"]:
                        eng.wait_ge(sems[semkey], val)
                    if rec["fn"] is None:
                        continue
                    ins = rec["fn"](eng)
                    if rec["dma"] is not None:
                        ins.then_inc(sems[("dma", rec["dma"][0])], rec["dma"][1])
                    elif rec["signal"]:
                        ins.then_inc(sems[("eng", ename)], 1)
            return body

        for ename, attr in engmap.items():
            if self.ops[ename]:
                getattr(block, attr)(make(ename))


class Arena:
    def __init__(self, ap, words):
        self.ap = ap
        self.words = words
        self.off = 0

    def reset(self):
        self.off = 0

    def alloc(self, free_elems, dtype):
        nbytes = free_elems * (2 if dtype == BF16 else 4)
        w = (nbytes + 3) // 4
        w = (w + 7) // 8 * 8
        assert self.off + w <= self.words, ("arena overflow", self.off, w, self.words)
        v = self.ap[:, self.off:self.off + w]
        self.off += w
        if dtype == BF16:
            v = v.bitcast(BF16)
        return v[:, 0:free_elems]


def build_program(S, stop_after=None):
    NT = S // 128
    NC5 = S // 512
    NB = S // 256
    NCH = S // 256
    assert S % 1024 == 0 and NB <= 16
    nc = bass.Bass("TRN2", target_bir_lowering=False)

    def din(name, shape, dt=F32):
        return nc.dram_tensor(name, shape, dt, kind="ExternalInput").ap()

    x = din("x", [S, D])
    cT = din("cT", [128, 8])
    ngT = din("ngT", [128, 8])
    badaT = din("badaT", [128, 24])
    wada = din("wada", [6, 128, 4096])
    wall = din("wall", [96, 128, 1024])
    wout = din("wout", [128, 8 * 1024])
    convw = din("convw", [128, 24])
    c31d = din("c31", [1, 8])
    btd = din("bt", [8, 128, 1024])
    fgd = din("fg", [1, 1024])
    out = nc.dram_tensor("out", [S, D], F32, kind="ExternalOutput").ap()
    aGd = nc.dram_tensor("aGd", [8, 128, S], BF16, kind="Internal").ap()
    cGd = nc.dram_tensor("cGd", [8, 128, S], BF16, kind="Internal").ap()
    dbg = None
    if stop_after is not None:
        dbg = nc.dram_tensor("dbg", [128, KC * S], BF16, kind="ExternalOutput").ap()

    st = ExitStack()
    with st:
        def sb(name, shape, dt):
            return st.enter_context(nc.sbuf_tensor(name, shape, dt))

        hT = sb("hT", [128, KC * S], BF16)
        hT3 = hT[:].rearrange("p (k s) -> p k s", k=KC)
        wring = sb("wring", [128, 8 * 1024], BF16)
        identf = sb("identf", [128, 128], F32)
        onesf = sb("onesf", [128, 128], F32)
        ident = sb("ident", [128, 128], BF16)
        onesb = sb("onesb", [128, 128], BF16)
        Emat = sb("Emat", [128, 16 * 128], BF16)
        Emat3 = Emat[:].rearrange("p (j c) -> p j c", j=16)
        pmask = sb("pmask", [128, NT * 16], F32)
        top8 = sb("top8", [128, NT * 8], F32)
        top83 = top8[:].rearrange("p (q e) -> p q e", e=8)
        fgb = sb("fgb", [128, 1024], F32)
        small = sb("small", [128, 256], F32)
        stat = sb("stat", [128, 4 * NT], F32)
        ARW = 26 * 1024
        arena_t = sb("arena", [128, ARW], F32)
        psum = st.enter_context(nc.psum_tensor("psum", [128, 4096], F32))
        psum_bf = psum[:].bitcast(BF16)
        AR = Arena(arena_t[:], ARW)
        P = Prog(nc)

        def finish_debug(src_ap):
            P.barrier()
            P.dma("sp", lambda e: e.dma_start(out=dbg[:, 0:src_ap.shape[1]], in_=src_ap), "dbg", writes=["dbg"])
            P.barrier()
            P.emit(st)

        def bank(i, a=0, b=512):
            return psum[:, i * 512 + a:i * 512 + b]

        def bk(i):
            return ("bank", i)

        cTs = small[:, 0:8]
        ngTs = small[:, 8:16]
        badas = small[:, 16:40]
        scs = small[:, 40:48]
        modT = small[:, 48:72]
        shiftT = small[:, 48:56]
        scaleT = small[:, 56:64]
        gateT = small[:, 64:72]
        g1T = small[:, 72:80]
        c31s = small[:, 80:88]
        cws = small[:, 88:112]
        kms = small[:, 112:128]
        epsc = small[:, 128:129]
        kmT = sb("kmT", [128, 16], BF16)

        def wr(s, i):
            return wring[:, (s * 4 + i) * 1024:(s * 4 + i + 1) * 1024]

        def load_w(s, i, cb):
            P.dma("pool", lambda e: e.dma_start(out=wr(s, i), in_=wall[cb]), f"wr{s}{i}", writes=[("wr", s, i)])

        for i, cb in enumerate((0, 8, 16, 24)):
            load_w(0, i, cb)
        P.dma("sp", lambda e: e.dma_start(out=cTs, in_=cT), "c0", writes=["cTs"])
        P.dma("sp", lambda e: e.dma_start(out=ngTs, in_=ngT), "c1", writes=["ngTs"])
        P.dma("sp", lambda e: e.dma_start(out=badas, in_=badaT), "c2", writes=["badas"])
        P.dma("sp", lambda e: e.dma_start(out=c31s, in_=c31d.to_broadcast([128, 8])), "c3", writes=["c31s"])
        P.dma("sp", lambda e: e.dma_start(out=cws, in_=convw), "c4", writes=["cws"])
        P.dma("sp", lambda e: e.dma_start(out=fgb[:], in_=fgd.to_broadcast([128, 1024])), "c5", writes=["fgb"])
        P.op("pool", lambda e: e.memset(identf[:], 1.0), writes=["identf"])
        P.op("pool", lambda e: e.affine_select(out=identf[:], in_=identf[:], pattern=[[-1, 128]],
                                               compare_op=ALU.is_equal, fill=0.0, base=0, channel_multiplier=1),
             reads=["identf"], writes=["identf"])
        P.op("pool", lambda e: e.memset(onesf[:], 1.0), writes=["onesf"])
        P.op("pool", lambda e: e.memset(onesb[:], 1.0), writes=["onesb"])
        P.op("pool", lambda e: e.memset(Emat[:], 1.0), writes=["Emat"])
        P.op("pool", lambda e: e.affine_select(out=Emat3, in_=Emat3, pattern=[[-1, 16], [0, 128]],
                                               compare_op=ALU.is_equal, fill=0.0, base=0, channel_multiplier=1),
             reads=["Emat"], writes=["Emat"])
        P.op("pool", lambda e: e.memset(pmask[:], -1e30), writes=["pmask"])
        pm4 = pmask[:].rearrange("p (b r j) -> p b r j", r=2, j=16)
        P.op("pool", lambda e: e.affine_select(out=pm4, in_=pm4, pattern=[[-1, NB], [0, 2], [1, 16]],
                                               compare_op=ALU.is_ge, fill=0.0, base=0, channel_multiplier=0),
             reads=["pmask"], writes=["pmask"])
        P.op("pool", lambda e: e.memset(top8[:], -1e29), writes=["top8"])
        P.op("pool", lambda e: e.memset(kmT[:], 0.0), writes=["kmT"])
        P.op("pool", lambda e: e.memset(epsc, EPS), writes=["epsc"])
        P.op("dve", lambda e: e.tensor_copy(out=ident[:], in_=identf[:]), reads=["identf"], writes=["ident"])

        AR.reset()
        QT = AR.alloc(S, BF16)
        KT = AR.alloc(S, BF16)
        GT = AR.alloc(S, BF16)
        VT = AR.alloc(S, BF16)
        V = AR.alloc(NT * 129 + 7, BF16)
        V3 = V[:, 0:NT * 129].rearrange("p (t d) -> p t d", d=129)
        at_ = [AR.alloc(128, F32) for _ in range(4)]
        rden = AR.alloc(8, F32)
        MnT = AR.alloc(S, BF16)
        scm = AR.alloc(NT * 16, F32)
        ltb = AR.alloc(NT * 16, F32)
        mneg = AR.alloc(NT * 16, BF16)
        mneg3 = mneg.rearrange("p (q j) -> p q j", j=16)
        BT = [AR.alloc(1024, F32) for _ in range(2)]
        NPT = 6
        Pt = [AR.alloc(512, BF16) for _ in range(NPT)]
        aGh = [AR.alloc(S, BF16) for _ in range(2)]

        rr = [0]

        def nextbank():
            b = rr[0] % 4
            rr[0] += 1
            return b

        def proj_chunk(which, tc, ws):
            w_ = wr(ws, (0, 1, 3, 2)[which])
            wkey = ("wr", ws, (0, 1, 3, 2)[which])
            b = nextbank()
            for kc in range(8):
                P.op("pe", lambda e, b=b, kc=kc, tc=tc, w_=w_: e.matmul(
                    bank(b), lhsT=w_[:, kc * 128:(kc + 1) * 128], rhs=hT3[:, kc, tc * 512:(tc + 1) * 512],
                    start=(kc == 0), stop=(kc == 7)),
                    reads=[wkey, ("hT", tc)], writes=[bk(b)], signal=(kc == 7))
            if which == 0:
                P.op("act", lambda e, b=b, tc=tc: e.activation(out=QT[:, tc * 512:(tc + 1) * 512], in_=bank(b), func=AF.Copy,
                                                               scale=float(HD ** -0.5)),
                     reads=[bk(b)], writes=[("QT", tc)])
            elif which == 1:
                for hb in range(2):
                    P.op("act", lambda e, b=b, tc=tc, hb=hb: e.activation(
                        out=KT[:, tc * 512 + hb * 256:tc * 512 + (hb + 1) * 256], in_=bank(b, hb * 256, (hb + 1) * 256),
                        func=AF.Copy, accum_out=kms[:, 2 * tc + hb:2 * tc + hb + 1]),
                        reads=[bk(b)], writes=[("KT", tc), ("kms", tc)])
            elif which == 2:
                P.op("act", lambda e, b=b, tc=tc: e.activation(out=GT[:, tc * 512:(tc + 1) * 512], in_=bank(b), func=AF.Silu),
                     reads=[bk(b)], writes=[("GT", tc)])
            else:
                P.op("act", lambda e, b=b, tc=tc: e.activation(out=VT[:, tc * 512:(tc + 1) * 512], in_=bank(b), func=AF.Copy),
                     reads=[bk(b)], writes=[("VT", tc)])

        P0_WORDS = 2 * 4096 + 4 * 1024 + 4 * 512 + 512 + 2 * 512
        assert AR.off <= ARW - P0_WORDS or True
        AR.off = ARW - P0_WORDS
        wa = [AR.alloc(4096, F32) for _ in range(2)]
        P.op("act", lambda e: e.activation(out=scs, in_=cTs, func=AF.Silu), reads=["cTs"], writes=["scs"])
        accm = [AR.alloc(512, F32) for _ in range(2)]
        for g in range(6):
            P.dma("sp", lambda e, g=g: e.dma_start(out=wa[g % 2], in_=wada[g]), f"wa{g % 2}", writes=[("wa", g % 2)])
            for kc in range(8):
                if kc == 0:
                    P.op("dve", lambda e, g=g: e.tensor_scalar(out=accm[g % 2], in0=wa[g % 2][:, 0:512], scalar1=scs[:, 0:1],
                                                               scalar2=None, op0=ALU.mult),
                         reads=[("wa", g % 2), "scs"], writes=[("accm", g % 2)])
                else:
                    P.op("dve", lambda e, g=g, kc=kc: e.scalar_tensor_tensor(
                        out=accm[g % 2], in0=wa[g % 2][:, kc * 512:(kc + 1) * 512], scalar=scs[:, kc:kc + 1], in1=accm[g % 2],
                        op0=ALU.mult, op1=ALU.add),
                        reads=[("wa", g % 2), "scs", ("accm", g % 2)], writes=[("accm", g % 2)])
            for j in range(4):
                col = g * 4 + j
                P.op("pe", lambda e, g=g, j=j, col=col: e.matmul(
                    bank(7, col, col + 1), lhsT=accm[g % 2][:, j * 128:(j + 1) * 128], rhs=onesf[:, 0:1], start=True, stop=True),
                    reads=[("accm", g % 2), "onesf"], writes=[bk(7)], signal=(j == 3))
        P.op("dve", lambda e: e.tensor_tensor(out=modT, in0=bank(7, 0, 24), in1=badas, op=ALU.add),
             reads=[bk(7), "badas"], writes=["modT"])
        P.op("dve", lambda e: e.scalar_tensor_tensor(out=g1T, in0=scaleT, scalar=1.0, in1=ngTs, op0=ALU.add, op1=ALU.mult),
             reads=["modT", "ngTs"], writes=["g1T"])

        if stop_after == "0a":
            P.op("dve", lambda e: e.tensor_copy(out=hT[:, 0:32], in_=small[:, 48:80]), reads=["modT", "g1T"], writes=["hTdbg"])
            finish_debug(hT[:, 0:32])
            return nc
        xt = [AR.alloc(1024, F32) for _ in range(4)]
        xn = [AR.alloc(1024, BF16) for _ in range(4)]
        junk = AR.alloc(1024, BF16)
        ss = stat[:, 0:NT]
        sq = stat[:, NT:2 * NT]
        rstd = stat[:, 2 * NT:3 * NT]

        def pT(s_, kc, a, b):
            base = (4 * s_ + kc // 2) * 1024 + (kc % 2) * 512
            return psum_bf[:, base + a:base + b]

        for tg in range(NC5):
            s_ = 1
            for i in range(4):
                tt = tg * 4 + i
                P.dma("sp", lambda e, tt=tt, i=i: e.dma_start(out=xt[i], in_=x[tt * 128:(tt + 1) * 128, :]),
                      f"xt{i}", writes=[("xt", i)])
                P.op("act", lambda e, tt=tt, i=i: e.activation(out=junk, in_=xt[i], func=AF.Square, accum_out=ss[:, tt:tt + 1]),
                     reads=[("xt", i)], writes=[("ss", tt), "junk"])
            P.op("act", lambda e, tg=tg: e.activation(out=sq[:, tg * 4:tg * 4 + 4], in_=ss[:, tg * 4:tg * 4 + 4], func=AF.Sqrt,
                                                      bias=epsc, scale=1.0 / D),
                 reads=[("ss", tg * 4 + i) for i in range(4)] + ["epsc"], writes=[("sq", tg)])
            P.op("dve", lambda e, tg=tg: e.reciprocal(out=rstd[:, tg * 4:tg * 4 + 4], in_=sq[:, tg * 4:tg * 4 + 4]),
                 reads=[("sq", tg)], writes=[("rstd", tg)])
            for i in range(4):
                tt = tg * 4 + i
                P.op("act", lambda e, tt=tt, i=i: e.activation(out=xn[i], in_=xt[i], func=AF.Copy, scale=rstd[:, tt:tt + 1]),
                     reads=[("xt", i), ("rstd", tg)], writes=[("xn", i)])
                for kc in range(8):
                    P.op("pe", lambda e, s_=s_, kc=kc, i=i: e.transpose(pT(s_, kc, i * 128, (i + 1) * 128),
                                                                         xn[i][:, kc * 128:(kc + 1) * 128], ident[:]),
                         reads=[("xn", i), "ident"], writes=[bk(4 * s_ + kc // 2)], signal=(kc == 7))
            for kc in range(8):
                P.op("dve", lambda e, s_=s_, kc=kc, tg=tg: e.tensor_scalar(
                    out=hT3[:, kc, tg * 512:(tg + 1) * 512], in0=pT(s_, kc, 0, 512),
                    scalar1=g1T[:, kc:kc + 1], scalar2=shiftT[:, kc:kc + 1], op0=ALU.mult, op1=ALU.add),
                    reads=[bk(4 * s_ + kc // 2), "g1T", "modT"], writes=[("hT", tg)])
            if tg >= 1:
                for which in range(4):
                    proj_chunk(which, tg - 1, 0)
        for which in range(4):
            proj_chunk(which, NC5 - 1, 0)
        all_hT = [("hT", tg) for tg in range(NC5)]

        P.op("pool", lambda e: e.memset(MnT, 0.0), reads=all_hT, writes=["MnT"])
        P.op("pool", lambda e: e.memset(V[:, 0:NT * 129], 1.0), reads=all_hT, writes=[("V", tg) for tg in range(NC5)])

        for h in range(H):
            ws = h % 2
            wq, wk, wv, wg = (wr(ws, i) for i in range(4))
            wkeys = [("wr", ws, i) for i in range(4)]
            ns = (h + 1) % 2
            if h + 1 < H:
                for i, cb in enumerate((h + 1, 8 + h + 1, 16 + h + 1, 24 + h + 1)):
                    load_w(ns, i, cb)
            else:
                for i, cb in enumerate((32, 40, 48, 56)):
                    load_w(ns, i, cb)
            P.dma("sp", lambda e, h=h: e.dma_start(out=BT[h % 2], in_=btd[h]), f"bt{h % 2}",
                  reads=(all_hT if h == 0 else []), writes=[("BT", h % 2)])
            def proj_fm(which):
                if h == 0:
                    return
                for tc in range(NC5):
                    proj_chunk(which, tc, ws)

            proj_fm(0)
            proj_fm(1)
            if h > 0:
                proj_chunk(2, 0, ws)
            P.op("dve", lambda e: e.tensor_scalar(out=kmT[:, 0:NB], in0=kms[:, 0:NB], scalar1=1.0 / 256, scalar2=None, op0=ALU.mult),
                 reads=[("kms", tc) for tc in range(NC5)] + ["kmT"], writes=["kmT"])
            scb = nextbank()
            for qt in range(NT):
                P.op("pe", lambda e, qt=qt, scb=scb: e.matmul(bank(scb, qt * 16, (qt + 1) * 16), lhsT=QT[:, qt * 128:(qt + 1) * 128],
                                                              rhs=kmT[:, 0:16], start=True, stop=True),
                     reads=[("QT", qt // 4), "kmT"], writes=[bk(scb)], signal=(qt == NT - 1))
            P.op("dve", lambda e, scb=scb: e.tensor_tensor(out=scm, in0=bank(scb, 0, NT * 16), in1=pmask[:], op=ALU.add),
                 reads=[bk(scb), "pmask"], writes=["scm"])
            for qt in range(8, NT):
                P.op("dve", lambda e, qt=qt: e.max(out=top83[:, qt, :], in_=scm[:, qt * 16:(qt + 1) * 16]),
                     reads=["scm"], writes=[("top8", qt)])
            scm3 = scm.rearrange("p (q j) -> p q j", j=16)
            ltb3 = ltb.rearrange("p (q j) -> p q j", j=16)
            P.op("dve", lambda e: e.tensor_tensor(out=ltb3, in0=scm3, in1=top83[:, :, 2:3].to_broadcast([128, NT, 16]), op=ALU.is_lt),
                 reads=["scm", "top8"] + [("top8", qt) for qt in range(8, NT)], writes=["ltb"])
            P.op("dve", lambda e: e.tensor_scalar(out=mneg, in0=ltb, scalar1=NEG, scalar2=None, op0=ALU.mult),
                 reads=["ltb"], writes=["mneg"])
            if h > 0:
                for tc in range(1, NC5):
                    proj_chunk(2, tc, ws)
            proj_fm(3)
            nbk = S // 1024
            for qt in range(NT):
                P.op("pe", lambda e, qt=qt: e.transpose(psum_bf[0:16, qt * 128:(qt + 1) * 128], mneg3[:, qt, :], ident[:]),
                     reads=["mneg", "ident"], writes=[bk(qt // 8)], signal=(qt % 8 == 7))
            for b in range(nbk):
                P.op("dve", lambda e, b=b: e.tensor_copy(out=MnT[0:16, b * 1024:(b + 1) * 1024],
                                                         in_=psum_bf[0:16, b * 1024:(b + 1) * 1024]),
                     reads=[bk(b)], writes=["MnT"])
            for tg in range(NC5):
                b = nextbank()
                for i in range(4):
                    tt = tg * 4 + i
                    P.op("pe", lambda e, b=b, i=i, tt=tt: e.transpose(psum_bf[:, b * 1024 + i * 128:b * 1024 + (i + 1) * 128],
                                                                      VT[:, tt * 128:(tt + 1) * 128], ident[:]),
                         reads=[("VT", tg), "ident"], writes=[bk(b)], signal=(i == 3))
                P.op("dve", lambda e, b=b, tg=tg: e.tensor_copy(
                    out=V3[:, tg * 4:(tg + 1) * 4, 0:128],
                    in_=psum_bf[:, b * 1024:b * 1024 + 512].rearrange("p (t d) -> p t d", d=128)),
                     reads=[bk(b)], writes=[("V", tg)])
            if stop_after == "A1":
                finish_debug(arena_t[:, 0:2 * S].bitcast(BF16))
                return nc
            if stop_after == "A2":
                finish_debug(MnT)
                return nc
            steps = [(qb, jb) for qb in range(NB) for jb in range(qb + 1)]
            LAG = 3
            bts = BT[h % 2]

            def qk(i):
                qb, jb = steps[i]
                sbk = i - (i // 4) * 4
                for kt in range(2):
                    P.op("pe", lambda e, kt=kt, qb=qb, jb=jb, sbk=sbk: e.matmul(
                        bank(sbk, kt * 256, (kt + 1) * 256), lhsT=KT[:, jb * 256 + kt * 128:jb * 256 + (kt + 1) * 128],
                        rhs=QT[:, qb * 256:(qb + 1) * 256], start=True, stop=(jb == qb or qb <= 3)),
                        reads=[("KT", jb // 2), ("QT", qb // 2)], writes=[bk(sbk)], signal=((jb == qb or qb <= 3) and kt == 1))
                    if jb < qb and qb > 3:
                        P.op("pe", lambda e, kt=kt, qb=qb, jb=jb, sbk=sbk: e.matmul(
                            bank(sbk, kt * 256, (kt + 1) * 256), lhsT=Emat3[:, jb, :],
                            rhs=MnT[:, qb * 256:(qb + 1) * 256], start=False, stop=True),
                            reads=["Emat", "MnT"], writes=[bk(sbk)], signal=(kt == 1))
                if jb >= qb - 1:
                    off = 0 if jb == qb else 512
                    P.op("dve", lambda e, sbk=sbk, off=off, bts=bts: e.tensor_tensor(out=bank(sbk), in0=bank(sbk), in1=bts[:, off:off + 512],
                                                                            op=ALU.add),
                         reads=[bk(sbk), ("BT", h % 2)], writes=[bk(sbk)])
                    P.op("act", lambda e, sbk=sbk, i=i: e.activation(out=Pt[i % NPT], in_=bank(sbk), func=AF.Exp),
                         reads=[bk(sbk)], writes=[("Pt", i % NPT)])
                else:
                    P.op("act", lambda e, sbk=sbk, i=i, h=h: e.activation(out=Pt[i % NPT], in_=bank(sbk), func=AF.Exp,
                                                                     bias=c31s[:, h:h + 1]),
                         reads=[bk(sbk), "c31s"], writes=[("Pt", i % NPT)])

                a2 = qb % 2
                if i in pool_steps[qb]:
                    if i == pool_steps[qb][0]:
                        P.op("pool", lambda e, i=i, a2=a2: e.tensor_copy(out=accP[a2], in_=Pt[i % NPT]),
                             reads=[("Pt", i % NPT)], writes=[("acc", a2)])
                    else:
                        P.op("pool", lambda e, i=i, a2=a2: e.tensor_tensor(out=accP[a2], in0=accP[a2], in1=Pt[i % NPT], op=ALU.add),
                             reads=[("Pt", i % NPT), ("acc", a2)], writes=[("acc", a2)])
                    if i == pool_steps[qb][-1]:
                        P.op("pool", lambda e, a2=a2: e.tensor_tensor(out=accF[a2], in0=accP[a2][:, 0:256], in1=accP[a2][:, 256:512],
                                                                      op=ALU.add),
                             reads=[("acc", a2)], writes=[("accF", a2)])
                        P.op("pool", lambda e, a2=a2: e.tensor_copy(out=dhi[a2], in_=accF[a2]),
                             reads=[("accF", a2)], writes=[("dhi", a2)])
                        P.op("pool", lambda e, a2=a2: e.tensor_tensor(out=dlo[a2], in0=accF[a2], in1=dhi[a2], op=ALU.subtract),
                             reads=[("accF", a2), ("dhi", a2)], writes=[("dlo", a2)])

            def obank(qb, qt2):
                return 4 + 2 * (qb % 2) + qt2

            def ocol(qt2):
                return 0

            def pv(i):
                qb, jb = steps[i]
                for qt2 in range(2):
                    ob = obank(qb, qt2)
                    for kt in range(2):
                        if jb == qb and kt == 1 and qt2 == 0:
                            continue
                        first = (jb == 0 and kt == 0)
                        last = (jb == qb and kt == 1) or (jb == qb and kt == 0 and qt2 == 0)
                        oc = ocol(qt2)
                        P.op("pe", lambda e, kt=kt, jb=jb, i=i, ob=ob, qt2=qt2, first=first, last=last, oc=oc: e.matmul(
                            bank(ob, oc, oc + 129), lhsT=Pt[i % NPT][:, kt * 256 + qt2 * 128:kt * 256 + (qt2 + 1) * 128],
                            rhs=V3[:, jb * 2 + kt, :], start=first, stop=last),
                            reads=[("V", jb // 2), ("Pt", i % NPT)], writes=[bk(ob)], signal=(qt2 == 1 and kt == 1))

            def finalize_a(qb):
                for qt2 in range(2):
                    ob = obank(qb, qt2)
                    a_ = at_[2 * (qb % 2) + qt2]
                    rc = rden[:, 2 * (qb % 2) + qt2:2 * (qb % 2) + qt2 + 1]
                    akey = ("at", 2 * (qb % 2) + qt2)
                    oc = ocol(qt2)
                    P.op("dve", lambda e, ob=ob, rc=rc, oc=oc: e.reciprocal(out=rc, in_=bank(ob, oc + 128, oc + 129)),
                         reads=[bk(ob)], writes=[("rden", qb % 2, qt2)])
                    P.op("dve", lambda e, ob=ob, rc=rc, a_=a_, oc=oc: e.tensor_scalar(out=a_, in0=bank(ob, oc, oc + 128), scalar1=rc, scalar2=None,
                                                                             op0=ALU.mult),
                         reads=[bk(ob), ("rden", qb % 2, qt2)], writes=[akey])

            def finalize(qb):
                tb0 = 0
                fb = obank(qb, 0)
                for qt2 in range(2):
                    a_ = at_[2 * (qb % 2) + qt2]
                    akey = ("at", 2 * (qb % 2) + qt2)
                    P.op("pe", lambda e, a_=a_, qt2=qt2, tb0=tb0, fb=fb: e.transpose(bank(fb, tb0 + qt2 * 128, tb0 + (qt2 + 1) * 128), a_,
                                                                                      identf[:]),
                         reads=[akey, "identf"], writes=[bk(fb)], signal=(qt2 == 1))
                P.op("dve", lambda e, qb=qb, tb0=tb0, h=h, fb=fb: e.tensor_tensor(out=aGh[h % 2][:, qb * 256:(qb + 1) * 256],
                                                                         in0=bank(fb, tb0, tb0 + 256),
                                                                         in1=GT[:, qb * 256:(qb + 1) * 256], op=ALU.mult),
                     reads=[bk(fb), ("GT", qb // 2)], writes=[("aGh", h % 2)])

            n = len(steps)
            pool_steps = {qb: [] for qb in range(NB)}
            pe_steps = {qb: [] for qb in range(NB)}
            for i_, (qb_, jb_) in enumerate(steps):
                (pool_steps if (POOL_DEN and i_ % POOL_DEN == POOL_DEN - 1) else pe_steps)[qb_].append(i_)
            pend = []
            for i in range(n + LAG):
                if i < n:
                    qk(i)
                if i - LAG >= 0:
                    if steps[i - LAG][1] == 0:
                        while pend and pend[0][0] <= steps[i - LAG][0] - 2:
                            finalize(pend.pop(0)[0])
                    pv(i - LAG)
                    while pend and pend[0][1] <= i:
                        finalize(pend.pop(0)[0])
                    if steps[i - LAG][1] == steps[i - LAG][0]:
                        finalize_a(steps[i - LAG][0])
                        pend.append((steps[i - LAG][0], i + 3))
            for qb_, _ in pend:
                finalize(qb_)
            P.dma("sp", lambda e, h=h: e.dma_start(out=aGd[h], in_=aGh[h % 2]), f"aGd{h % 2}",
                  reads=[("aGh", h % 2)], writes=[("aGd", h)])
            if stop_after == "A3":
                finish_debug(aGh[0])
                return nc
        P.barrier()

        AR.reset()
        PB_WORDS = ((S + 2 + 7) // 8 * 8) + 8 * 512 + 2 * (S // 2)
        AR.off = (ARW - PB_WORDS) // 8 * 8
        N_PRE = min(32, AR.off // 512)
        order = []
        for j in range(8):
            order += [(j, 80 + j), (8 + j, 88 + j), (16 + j, 64 + j), (24 + j, 72 + j)]
        Wslot = {wi: arena_t[:, k * 512:(k + 1) * 512].bitcast(BF16) for k, (wi, cb) in enumerate(order)}

        def load_wj(k):
            wi, cb = order[k]
            P.dma("pool", lambda e, wi=wi, cb=cb: e.dma_start(out=Wslot[wi], in_=wall[cb]), f"wj{wi}", writes=[("Wj", wi)])

        u = AR.alloc(S + 2, F32)
        cxs = [AR.alloc(512, F32) for _ in range(2)]
        ybuf = [AR.alloc(512, F32) for _ in range(2)]
        sgb = [AR.alloc(512, F32) for _ in range(2)]
        zb = [AR.alloc(512, F32) for _ in range(2)]
        cGs = [AR.alloc(S, BF16) for _ in range(2)]
        P.op("pool", lambda e: e.memset(u[:, 0:2], 0.0), writes=[("u", -1)])
        for c in range(8):
            ws = c % 2
            wcb, wcc, wcx, wgc = (wr(ws, i) for i in range(4))
            wkeys = [("wr", ws, i) for i in range(4)]
            if c + 1 < 8:
                for i, cb in enumerate((32 + c + 1, 40 + c + 1, 48 + c + 1, 56 + c + 1)):
                    load_w((c + 1) % 2, i, cb)
            if c == 1:
                for k in range(N_PRE):
                    load_wj(k)
            for tc in range(NC5):
                for wi, w_ in enumerate((wcb, wcc, wcx, wgc)):
                    for kc in range(8):
                        P.op("pe", lambda e, wi=wi, w_=w_, kc=kc, tc=tc: e.matmul(
                            bank(wi + 4 * (tc % 2)), lhsT=w_[:, kc * 128:(kc + 1) * 128], rhs=hT3[:, kc, tc * 512:(tc + 1) * 512],
                            start=(kc == 0), stop=(kc == 7)),
                            reads=[wkeys[wi], "hT"], writes=[bk(wi + 4 * (tc % 2))], signal=(kc == 7))
                o = 4 * (tc % 2)
                k2 = tc % 2
                P.op("act", lambda e, o=o, k2=k2: e.activation(out=cxs[k2], in_=bank(o + 2), func=AF.Copy),
                     reads=[bk(o + 2)], writes=[("cxs", k2)])
                P.op("dve", lambda e, o=o, k2=k2, tc=tc: e.tensor_tensor(out=u[:, 2 + tc * 512:2 + (tc + 1) * 512], in0=bank(o + 1),
                                                                         in1=cxs[k2], op=ALU.mult),
                     reads=[bk(o + 1), ("cxs", k2)], writes=[("u", tc)])
                P.op("dve", lambda e, k2=k2, tc=tc, c=c: e.tensor_scalar(out=ybuf[k2], in0=u[:, 2 + tc * 512:2 + (tc + 1) * 512],
                                                                         scalar1=cws[:, c * 3 + 2:c * 3 + 3], scalar2=None, op0=ALU.mult),
                     reads=[("u", tc), "cws"], writes=[("y", k2)])
                P.op("dve", lambda e, k2=k2, tc=tc, c=c: e.scalar_tensor_tensor(
                    out=ybuf[k2], in0=u[:, 1 + tc * 512:1 + (tc + 1) * 512], scalar=cws[:, c * 3 + 1:c * 3 + 2], in1=ybuf[k2],
                    op0=ALU.mult, op1=ALU.add),
                    reads=[("u", tc), ("u", tc - 1), ("y", k2)], writes=[("y", k2)])
                P.op("dve", lambda e, k2=k2, tc=tc, c=c: e.scalar_tensor_tensor(
                    out=ybuf[k2], in0=u[:, tc * 512:(tc + 1) * 512], scalar=cws[:, c * 3:c * 3 + 1], in1=ybuf[k2],
                    op0=ALU.mult, op1=ALU.add),
                    reads=[("u", tc), ("u", tc - 1), ("y", k2)], writes=[("y", k2)])
                P.op("act", lambda e, o=o, k2=k2: e.activation(out=sgb[k2], in_=bank(o + 3), func=AF.Silu),
                     reads=[bk(o + 3)], writes=[("sg", k2)])
                P.op("dve", lambda e, o=o, k2=k2: e.tensor_tensor(out=zb[k2], in0=bank(o), in1=ybuf[k2], op=ALU.mult),
                     reads=[bk(o), ("y", k2)], writes=[("z", k2)])
                P.op("dve", lambda e, k2=k2, tc=tc, c=c: e.tensor_tensor(out=cGs[c % 2][:, tc * 512:(tc + 1) * 512], in0=zb[k2],
                                                                         in1=sgb[k2], op=ALU.mult),
                     reads=[("z", k2), ("sg", k2)], writes=[("cGs", c % 2)])
            P.dma("sp", lambda e, c=c: e.dma_start(out=cGd[c], in_=cGs[c % 2]), f"cGd{c % 2}",
                  reads=[("cGs", c % 2)], writes=[("cGd", c)])
        P.barrier()

        AR.reset()
        AR.off = 32 * 512
        Wj = [Wslot[wi] for wi in range(32)]
        aGc = [AR.alloc(8 * 256, BF16) for _ in range(2)]
        cGc = [AR.alloc(8 * 256, BF16) for _ in range(2)]
        sgm = [AR.alloc(512, F32) for _ in range(2)]
        tmp = [AR.alloc(512, F32) for _ in range(2)]
        mst = [AR.alloc(8 * 256, BF16) for _ in range(2)]
        for k in range(N_PRE, 32):
            load_wj(k)
        Wg3 = wring[:].rearrange("p (k n) -> p k n", k=8)
        gbc = AR.alloc(1024, F32)
        wst1 = AR.alloc(1024, F32)
        for kc in range(8):
            P.op("dve", lambda e, kc=kc: e.tensor_scalar(out=wst1[:, kc * 128:(kc + 1) * 128], in0=identf[:], scalar1=gateT[:, kc:kc + 1],
                                                         scalar2=None, op0=ALU.mult),
                 writes=[("Dk", kc)])
            P.op("pe", lambda e, kc=kc: e.matmul(bank(4 + kc // 4, (kc % 4) * 128, (kc % 4 + 1) * 128), lhsT=onesf[:],
                                                 rhs=wst1[:, kc * 128:(kc + 1) * 128], start=True, stop=True),
                 reads=[("Dk", kc)], writes=[bk(4 + kc // 4)], signal=(kc % 4 == 3))
        P.op("dve", lambda e: e.tensor_copy(out=gbc, in_=psum[:, 2048:3072]), reads=[bk(4), bk(5)], writes=["gbc"])
        wg_done = [0]

        def wg_piece(kc):
            P.dma("sp", lambda e, kc=kc: e.dma_start(out=wst1, in_=wout[:, kc * 1024:(kc + 1) * 1024]), "wst1",
                  writes=["wst1"] + [("Dk", k_) for k_ in range(8)])
            P.op("dve", lambda e, kc=kc: e.tensor_tensor(out=Wg3[:, kc, :], in0=wst1, in1=gbc, op=ALU.mult),
                 reads=["wst1", "gbc"], writes=["Wg"])
        aGd_v = aGd.rearrange("h p s -> p h s")
        cGd_v = cGd.rearrange("h p s -> p h s")
        for t in range(NCH):
            k2 = t % 2
            a3 = aGc[k2].rearrange("p (h s) -> p h s", h=8)
            c3 = cGc[k2].rearrange("p (h s) -> p h s", h=8)
            m3 = mst[k2].rearrange("p (h s) -> p h s", h=8)
            P.dma("sp", lambda e, t=t, a3=a3: e.dma_start(out=a3, in_=aGd_v[:, :, t * 256:(t + 1) * 256]), f"aGc{k2}",
                  writes=[("aGc", k2)])
            P.dma("sp", lambda e, t=t, c3=c3: e.dma_start(out=c3, in_=cGd_v[:, :, t * 256:(t + 1) * 256]), f"cGc{k2}",
                  writes=[("cGc", k2)])
            if 2 <= t < 10:
                wg_piece(t - 2)
                wg_done[0] = t - 1
            for j in range(8):
                idx = t * 8 + j
                bA, bB = 2 * (idx % 2), 2 * (idx % 2) + 1
                groups = [(bA, 0, Wj[j], ("Wj", j), a3, ("aGc", k2)),
                          (bA, 256, Wj[8 + j], ("Wj", 8 + j), c3, ("cGc", k2)),
                          (bB, 0, Wj[16 + j], ("Wj", 16 + j), None, None),
                          (bB, 256, Wj[24 + j], ("Wj", 24 + j), None, None)]
                for gi, (bb, off, w_, wkey, src, skey) in enumerate(groups):
                    for kc in range(8):
                        if src is None:
                            rhs = hT3[:, kc, t * 256:(t + 1) * 256]
                            rk = ("hT", t)
                        else:
                            rhs = src[:, kc, :]
                            rk = skey
                        P.op("pe", lambda e, bb=bb, off=off, w_=w_, kc=kc, rhs=rhs: e.matmul(
                            bank(bb, off, off + 256), lhsT=w_[:, kc * 128:(kc + 1) * 128], rhs=rhs,
                            start=(kc == 0), stop=(kc == 7)),
                            reads=[wkey, rk], writes=[bk(bb)], signal=(kc == 7 and gi % 2 == 1))
                i2 = idx % 2
                P.op("act", lambda e, bB=bB, i2=i2: e.activation(out=sgm[i2], in_=bank(bB), func=AF.Sigmoid),
                     reads=[bk(bB)], writes=[("sgm", i2)])
                P.op("dve", lambda e, bA=bA, i2=i2: e.tensor_tensor(out=tmp[i2], in0=bank(bA), in1=sgm[i2], op=ALU.mult),
                     reads=[bk(bA), ("sgm", i2)], writes=[("tmp", i2)])
                P.op("dve", lambda e, i2=i2, j=j, m3=m3: e.tensor_tensor(out=m3[:, j, :], in0=tmp[i2][:, 0:256], in1=tmp[i2][:, 256:512],
                                                                         op=ALU.add),
                     reads=[("tmp", i2)], writes=[("mst", k2)])
            P.op("pool", lambda e, t=t, m3=m3: e.tensor_copy(out=hT3[:, :, t * 256:(t + 1) * 256], in_=m3),
                 reads=[("mst", k2)], writes=[("hT", t)])
        for kc in range(wg_done[0], 8):
            wg_piece(kc)
        P.barrier()

        AR.reset()
        xt2 = [AR.alloc(1024, F32) for _ in range(4)]
        rb = [AR.alloc(1024, F32) for _ in range(3)]
        ot = [AR.alloc(1024, F32) for _ in range(4)]
        junk2 = AR.alloc(1024, BF16)
        ss2 = stat[:, 0:NT]
        sq2 = stat[:, NT:2 * NT]
        rstd2 = stat[:, 2 * NT:3 * NT]
        out_keys = []
        def load_x2(tt):
            P.dma("sp", lambda e, tt=tt: e.dma_start(out=xt2[tt % 4], in_=x[tt * 128:(tt + 1) * 128, :]), f"xt2{tt % 4}",
                  writes=[("xt2", tt % 4)])

        PF = 3
        for tt in range(min(PF, NT)):
            load_x2(tt)
        for tt in range(NT):
            k4 = tt % 4
            k3 = tt % 3
            if tt + PF < NT:
                load_x2(tt + PF)
            for half in range(2):
                bb = 2 * k4 + half
                for kc in range(8):
                    P.op("pe", lambda e, bb=bb, kc=kc, tt=tt, half=half: e.matmul(
                        bank(bb), lhsT=hT3[:, kc, tt * 128:(tt + 1) * 128], rhs=Wg3[:, kc, half * 512:(half + 1) * 512],
                        start=(kc == 0), stop=(kc == 7)),
                        reads=[], writes=[bk(bb)], signal=(kc == 7 and half == 1))
            P.op("dve", lambda e, k4=k4, k3=k3: e.tensor_tensor(out=rb[k3], in0=psum[:, (2 * k4) * 512:(2 * k4) * 512 + 1024],
                                                         in1=xt2[k4], op=ALU.add),
                 reads=[bk(2 * k4), bk(2 * k4 + 1), ("xt2", k4)], writes=[("rb", k3)])
            P.op("act", lambda e, k3=k3, tt=tt: e.activation(out=junk2, in_=rb[k3], func=AF.Square, accum_out=ss2[:, tt:tt + 1]),
                 reads=[("rb", k3)], writes=[("ss2", tt), "junk2"])
            P.op("act", lambda e, tt=tt: e.activation(out=sq2[:, tt:tt + 1], in_=ss2[:, tt:tt + 1], func=AF.Sqrt, bias=epsc,
                                                      scale=1.0 / D),
                 reads=[("ss2", tt), "epsc"], writes=[("sq2", tt)])
            P.op("dve", lambda e, tt=tt: e.reciprocal(out=rstd2[:, tt:tt + 1], in_=sq2[:, tt:tt + 1]),
                 reads=[("sq2", tt)], writes=[("rstd2", tt)])
            P.op("dve", lambda e, k3=k3, k4=k4, tt=tt: e.scalar_tensor_tensor(out=ot[k4], in0=rb[k3], scalar=rstd2[:, tt:tt + 1], in1=fgb[:],
                                                                       op0=ALU.mult, op1=ALU.mult),
                 reads=[("rb", k3), ("rstd2", tt), "fgb"], writes=[("ot", k4)])
            P.dma("sp", lambda e, tt=tt, k4=k4: e.dma_start(out=out[tt * 128:(tt + 1) * 128, :], in_=ot[k4]), f"out{k4}",
                  reads=[("ot", k4)], writes=[("out", tt)])
            out_keys.append(("out", tt))
        P.barrier()
        P.emit(st)
    return nc


def _bucket_table(n):
    d = np.arange(n)
    nf = np.maximum(d, 1).astype(np.float32)
    large = 16 + (np.log(nf / np.float32(16)) / np.float32(np.log(128 / 16)) * np.float32(16)).astype(np.int32)
    large = np.minimum(large, 31)
    return np.where(d < 16, d, large)


def prep_shared(norm_g, w_ada, b_ada, w_in, conv_w, w_o_attn, w_o_conv, w_out, rel_bias, final_g):
    f = np.float32
    sh = {}
    sh["ngT"] = np.ascontiguousarray(norm_g[0].reshape(8, 128).T).astype(f)
    sh["badaT"] = np.ascontiguousarray(b_ada[0].reshape(24, 128).T).astype(f)
    sh["wada"] = np.ascontiguousarray(w_ada[0].reshape(8, 128, 6, 512).transpose(2, 1, 0, 3)).reshape(6, 128, 4096)

    def colblocks(w):
        ncb = w.shape[1] // 128
        return np.ascontiguousarray(w.reshape(8, 128, ncb, 128).transpose(2, 1, 0, 3)).reshape(ncb, 128, 1024)

    sh["wall"] = np.concatenate([colblocks(w_in[0]), colblocks(w_o_attn[0]), colblocks(w_o_conv[0])], axis=0)
    sh["wout"] = np.ascontiguousarray(w_out[0].reshape(8, 128, 1024).transpose(1, 0, 2)).reshape(128, 8192)
    sh["convw"] = np.ascontiguousarray(conv_w[0].T.reshape(8, 128, 3).transpose(1, 0, 2)).reshape(128, 24)
    sh["c31"] = np.ascontiguousarray(rel_bias[31:32, :]).astype(f)
    sh["fg"] = np.ascontiguousarray(final_g.reshape(1, 1024)).astype(f)
    bkt = _bucket_table(512)
    k = np.arange(256)[:, None]
    q = np.arange(256)[None, :]
    d_own = q - k
    d_prev = q + 256 - k
    bt = np.empty((8, 128, 1024), f)
    for h in range(8):
        tb = rel_bias[:, h]
        own = np.where(d_own >= 0, tb[bkt[np.maximum(d_own, 0)]], f(NEG)).astype(f)
        prev = tb[bkt[d_prev]].astype(f)
        bt[h, :, 0:512] = own.reshape(2, 128, 256).transpose(1, 0, 2).reshape(128, 512)
        bt[h, :, 512:1024] = prev.reshape(2, 128, 256).transpose(1, 0, 2).reshape(128, 512)
    sh["bt"] = bt
    return sh


_CACHE = {}


def _get_prog(S):
    if S not in _CACHE:
        _CACHE[S] = build_program(S)
    return _CACHE[S]


def kernel(x, c, norm_g, w_ada, b_ada, w_in, conv_w, w_o_attn, w_o_conv, w_out, rel_bias, final_g):
    x = np.asarray(x, np.float32)
    c = np.asarray(c, np.float32)
    B, S, _ = x.shape
    sh = prep_shared(*(np.asarray(a, np.float32) for a in
                       (norm_g, w_ada, b_ada, w_in, conv_w, w_o_attn, w_o_conv, w_out, rel_bias, final_g)))
    nc = _get_prog(S)
    in_maps = []
    for b in range(B):
        m = dict(sh)
        m["x"] = np.ascontiguousarray(x[b])
        m["cT"] = np.ascontiguousarray(c[b].reshape(8, 128).T)
        in_maps.append(m)
    res = run_bass_kernel_spmd(nc, in_maps, core_ids=list(range(B)))
    return np.stack([np.asarray(r["out"], np.float32) for r in res.results], axis=0)
```

```python
import bisect
import numpy as np
from contextlib import ExitStack
import concourse.bass as bass
import concourse.mybir as mybir
from concourse.bass_utils import run_bass_kernel_spmd

F32 = mybir.dt.float32
BF16 = mybir.dt.bfloat16
AF = mybir.ActivationFunctionType
ALU = mybir.AluOpType
AX = mybir.AxisListType

D = 1024
KC = 8
H = 8
HD = 128
NEG = -30000.0
EPS = 1e-6
COMPUTE = ("pe", "act", "dve", "pool")
STRICT = ("act", "dve", "pool")
POOL_DEN = 0


class Prog:
    def __init__(self, nc):
        self.nc = nc
        self.ops = {e: [] for e in ("pe", "act", "dve", "pool", "sp")}
        self.sigseqs = {e: [] for e in COMPUTE}
        self.waited = {e: {} for e in self.ops}
        self.state = {}
        self.dma_cnt = {}

    def _resolve(self, dep):
        if dep[0] == "dma":
            return (("dma", dep[1]), dep[2])
        _, eng, seq = dep
        lst = self.sigseqs[eng]
        i = bisect.bisect_left(lst, seq)
        if i == len(lst):
            last = len(self.ops[eng]) - 1
            while self.ops[eng][last]["fn"] is None or self.ops[eng][last]["dma"] is not None:
                last -= 1
            assert last >= seq, (eng, seq, last)
            rec = self.ops[eng][last]
            assert not rec["signal"]
            rec["signal"] = True
            lst.append(last)
            i = len(lst) - 1
        return (("eng", eng), i + 1)

    def _collect(self, eng, reads, writes):
        deps = []
        for k in reads:
            st = self.state.get(k)
            if st and st[0] is not None:
                deps.append(st[0])
        for k in writes:
            st = self.state.get(k)
            if st:
                if st[0] is not None:
                    deps.append(st[0])
                deps.extend(st[1].values())
        waits = {}
        for d in deps:
            if d[0] == "eng" and d[1] == eng and eng not in STRICT:
                continue
            semkey, val = self._resolve(d)
            if self.waited[eng].get(semkey, 0) >= val:
                continue
            if waits.get(semkey, 0) < val:
                waits[semkey] = val
        for semkey, val in waits.items():
            self.waited[eng][semkey] = val
        return list(waits.items())

    def _update(self, me, reads, writes):
        for k in reads:
            st = self.state.setdefault(k, [None, {}])
            st[1][(me[0], me[1])] = me
        for k in writes:
            self.state[k] = [me, {}]

    def op(self, eng, fn, reads=(), writes=(), signal=True):
        waits = self._collect(eng, reads, writes)
        seq = len(self.ops[eng])
        self.ops[eng].append({"fn": fn, "waits": waits, "signal": signal, "dma": None})
        if signal:
            self.sigseqs[eng].append(seq)
        self._update(("eng", eng, seq), reads, writes)

    def dma(self, queue, fn, sem, reads=(), writes=(), inc=16):
        waits = self._collect(queue, reads, writes)
        val = self.dma_cnt.get(sem, 0) + inc
        self.dma_cnt[sem] = val
        self.ops[queue].append({"fn": fn, "waits": waits, "signal": False, "dma": (sem, inc)})
        self._update(("dma", sem, val), reads, writes)

    def barrier(self):
        targets = []
        for e in COMPUTE:
            n = len(self.ops[e])
            last = n - 1
            while last >= 0 and (self.ops[e][last]["fn"] is None or self.ops[e][last]["dma"] is not None):
                last -= 1
            if last >= 0:
                targets.append(self._resolve(("eng", e, last)))
        for name, val in self.dma_cnt.items():
            targets.append((("dma", name), val))
        for e in self.ops:
            waits = []
            for semkey, val in targets:
                if semkey == ("eng", e):
                    continue
                if self.waited[e].get(semkey, 0) >= val:
                    continue
                self.waited[e][semkey] = val
                waits.append((semkey, val))
            if waits:
                self.ops[e].append({"fn": None, "waits": waits, "signal": False, "dma": None})
        self.state = {}

    def emit(self, stack):
        nc = self.nc
        sems = {}
        for e in COMPUTE:
            sems[("eng", e)] = stack.enter_context(nc.semaphore(f"s_{e}"))
        for name in self.dma_cnt:
            sems[("dma", name)] = stack.enter_context(nc.semaphore(f"d_{name}"))
        block = stack.enter_context(nc.Block())
        engmap = {"pe": "tensor", "act": "scalar", "dve": "vector", "pool": "gpsimd", "sp": "sync"}

        def make(ename):
            recs = self.ops[ename]

            def body(eng):
                for rec in recs:
                    for semkey, val in rec["waits"]:
                        eng.wait_ge(sems[semkey], val)
                    if rec["fn"] is None:
                        continue
                    ins = rec["fn"](eng)
                    if rec["dma"] is not None:
                        ins.then_inc(sems[("dma", rec["dma"][0])], rec["dma"][1])
                    elif rec["signal"]:
                        ins.then_inc(sems[("eng", ename)], 1)
            return body

        for ename, attr in engmap.items():
            if self.ops[ename]:
                getattr(block, attr)(make(ename))


class Arena:
    def __init__(self, ap, words):
        self.ap = ap
        self.words = words
        self.off = 0

    def reset(self):
        self.off = 0

    def alloc(self, free_elems, dtype):
        nbytes = free_elems * (2 if dtype == BF16 else 4)
        w = (nbytes + 3) // 4
        w = (w + 7) // 8 * 8
        assert self.off + w <= self.words, ("arena overflow", self.off, w, self.words)
        v = self.ap[:, self.off:self.off + w]
        self.off += w
        if dtype == BF16:
            v = v.bitcast(BF16)
        return v[:, 0:free_elems]


def build_program(S, stop_after=None):
    NT = S // 128
    NC5 = S // 512
    NB = S // 256
    NCH = S // 256
    assert S % 1024 == 0 and NB <= 16
    nc = bass.Bass("TRN2", target_bir_lowering=False)

    def din(name, shape, dt=F32):
        return nc.dram_tensor(name, shape, dt, kind="ExternalInput").ap()

    x = din("x", [S, D])
    cT = din("cT", [128, 8])
    ngT = din("ngT", [128, 8])
    badaT = din("badaT", [128, 24])
    wada = din("wada", [6, 128, 4096])
    wall = din("wall", [96, 128, 1024])
    wout = din("wout", [128, 8 * 1024])
    convw = din("convw", [128, 24])
    c31d = din("c31", [1, 8])
    btd = din("bt", [8, 128, 1024])
    fgd = din("fg", [1, 1024])
    out = nc.dram_tensor("out", [S, D], F32, kind="ExternalOutput").ap()
    aGd = nc.dram_tensor("aGd", [8, 128, S], BF16, kind="Internal").ap()
    cGd = nc.dram_tensor("cGd", [8, 128, S], BF16, kind="Internal").ap()
    dbg = None
    if stop_after is not None:
        dbg = nc.dram_tensor("dbg", [128, KC * S], BF16, kind="ExternalOutput").ap()

    st = ExitStack()
    with st:
        def sb(name, shape, dt):
            return st.enter_context(nc.sbuf_tensor(name, shape, dt))

        hT = sb("hT", [128, KC * S], BF16)
        hT3 = hT[:].rearrange("p (k s) -> p k s", k=KC)
        wring = sb("wring", [128, 8 * 1024], BF16)
        identf = sb("identf", [128, 128], F32)
        onesf = sb("onesf", [128, 128], F32)
        ident = sb("ident", [128, 128], BF16)
        onesb = sb("onesb", [128, 128], BF16)
        Emat = sb("Emat", [128, 16 * 128], BF16)
        Emat3 = Emat[:].rearrange("p (j c) -> p j c", j=16)
        pmask = sb("pmask", [128, NT * 16], F32)
        top8 = sb("top8", [128, NT * 8], F32)
        top83 = top8[:].rearrange("p (q e) -> p q e", e=8)
        fgb = sb("fgb", [128, 1024], F32)
        small = sb("small", [128, 256], F32)
        stat = sb("stat", [128, 4 * NT], F32)
        ARW = 26 * 1024
        arena_t = sb("arena", [128, ARW], F32)
        psum = st.enter_context(nc.psum_tensor("psum", [128, 4096], F32))
        psum_bf = psum[:].bitcast(BF16)
        AR = Arena(arena_t[:], ARW)
        P = Prog(nc)

        def finish_debug(src_ap):
            P.barrier()
            P.dma("sp", lambda e: e.dma_start(out=dbg[:, 0:src_ap.shape[1]], in_=src_ap), "dbg", writes=["dbg"])
            P.barrier()
            P.emit(st)

        def bank(i, a=0, b=512):
            return psum[:, i * 512 + a:i * 512 + b]

        def bk(i):
            return ("bank", i)

        cTs = small[:, 0:8]
        ngTs = small[:, 8:16]
        badas = small[:, 16:40]
        scs = small[:, 40:48]
        modT = small[:, 48:72]
        shiftT = small[:, 48:56]
        scaleT = small[:, 56:64]
        gateT = small[:, 64:72]
        g1T = small[:, 72:80]
        c31s = small[:, 80:88]
        cws = small[:, 88:112]
        kms = small[:, 112:128]
        epsc = small[:, 128:129]
        kmT = sb("kmT", [128, 16], BF16)

        def wr(s, i):
            return wring[:, (s * 4 + i) * 1024:(s * 4 + i + 1) * 1024]

        def load_w(s, i, cb):
            P.dma("pool", lambda e: e.dma_start(out=wr(s, i), in_=wall[cb]), f"wr{s}{i}", writes=[("wr", s, i)])

        for i, cb in enumerate((0, 8, 16, 24)):
            load_w(0, i, cb)
        P.dma("sp", lambda e: e.dma_start(out=cTs, in_=cT), "c0", writes=["cTs"])
        P.dma("sp", lambda e: e.dma_start(out=ngTs, in_=ngT), "c1", writes=["ngTs"])
        P.dma("sp", lambda e: e.dma_start(out=badas, in_=badaT), "c2", writes=["badas"])
        P.dma("sp", lambda e: e.dma_start(out=c31s, in_=c31d.to_broadcast([128, 8])), "c3", writes=["c31s"])
        P.dma("sp", lambda e: e.dma_start(out=cws, in_=convw), "c4", writes=["cws"])
        P.dma("sp", lambda e: e.dma_start(out=fgb[:], in_=fgd.to_broadcast([128, 1024])), "c5", writes=["fgb"])
        P.op("pool", lambda e: e.memset(identf[:], 1.0), writes=["identf"])
        P.op("pool", lambda e: e.affine_select(out=identf[:], in_=identf[:], pattern=[[-1, 128]],
                                               compare_op=ALU.is_equal, fill=0.0, base=0, channel_multiplier=1),
             reads=["identf"], writes=["identf"])
        P.op("pool", lambda e: e.memset(onesf[:], 1.0), writes=["onesf"])
        P.op("pool", lambda e: e.memset(onesb[:], 1.0), writes=["onesb"])
        P.op("pool", lambda e: e.memset(Emat[:], 1.0), writes=["Emat"])
        P.op("pool", lambda e: e.affine_select(out=Emat3, in_=Emat3, pattern=[[-1, 16], [0, 128]],
                                               compare_op=ALU.is_equal, fill=0.0, base=0, channel_multiplier=1),
             reads=["Emat"], writes=["Emat"])
        P.op("pool", lambda e: e.memset(pmask[:], -1e30), writes=["pmask"])
        pm4 = pmask[:].rearrange("p (b r j) -> p b r j", r=2, j=16)
        P.op("pool", lambda e: e.affine_select(out=pm4, in_=pm4, pattern=[[-1, NB], [0, 2], [1, 16]],
                                               compare_op=ALU.is_ge, fill=0.0, base=0, channel_multiplier=0),
             reads=["pmask"], writes=["pmask"])
        P.op("pool", lambda e: e.memset(top8[:], -1e29), writes=["top8"])
        P.op("pool", lambda e: e.memset(kmT[:], 0.0), writes=["kmT"])
        P.op("pool", lambda e: e.memset(epsc, EPS), writes=["epsc"])
        P.op("dve", lambda e: e.tensor_copy(out=ident[:], in_=identf[:]), reads=["identf"], writes=["ident"])

        AR.reset()
        QT = AR.alloc(S, BF16)
        KT = AR.alloc(S, BF16)
        GT = AR.alloc(S, BF16)
        VT = AR.alloc(S, BF16)
        V = AR.alloc(NT * 129 + 7, BF16)
        V3 = V[:, 0:NT * 129].rearrange("p (t d) -> p t d", d=129)
        at_ = [AR.alloc(128, F32) for _ in range(4)]
        rden = AR.alloc(8, F32)
        MnT = AR.alloc(S, BF16)
        scm = AR.alloc(NT * 16, F32)
        ltb = AR.alloc(NT * 16, F32)
        mneg = AR.alloc(NT * 16, BF16)
        mneg3 = mneg.rearrange("p (q j) -> p q j", j=16)
        BT = [AR.alloc(1024, F32) for _ in range(2)]
        NPT = 6
        Pt = [AR.alloc(512, BF16) for _ in range(NPT)]
        aGh = [AR.alloc(S, BF16) for _ in range(2)]

        rr = [0]

        def nextbank():
            b = rr[0] % 4
            rr[0] += 1
            return b

        def proj_chunk(which, tc, ws):
            w_ = wr(ws, (0, 1, 3, 2)[which])
            wkey = ("wr", ws, (0, 1, 3, 2)[which])
            b = nextbank()
            for kc in range(8):
                P.op("pe", lambda e, b=b, kc=kc, tc=tc, w_=w_: e.matmul(
                    bank(b), lhsT=w_[:, kc * 128:(kc + 1) * 128], rhs=hT3[:, kc, tc * 512:(tc + 1) * 512],
                    start=(kc == 0), stop=(kc == 7)),
                    reads=[wkey, ("hT", tc)], writes=[bk(b)], signal=(kc == 7))
            if which == 0:
                P.op("act", lambda e, b=b, tc=tc: e.activation(out=QT[:, tc * 512:(tc + 1) * 512], in_=bank(b), func=AF.Copy,
                                                               scale=float(HD ** -0.5)),
                     reads=[bk(b)], writes=[("QT", tc)])
            elif which == 1:
                for hb in range(2):
                    P.op("act", lambda e, b=b, tc=tc, hb=hb: e.activation(
                        out=KT[:, tc * 512 + hb * 256:tc * 512 + (hb + 1) * 256], in_=bank(b, hb * 256, (hb + 1) * 256),
                        func=AF.Copy, accum_out=kms[:, 2 * tc + hb:2 * tc + hb + 1]),
                        reads=[bk(b)], writes=[("KT", tc), ("kms", tc)])
            elif which == 2:
                P.op("act", lambda e, b=b, tc=tc: e.activation(out=GT[:, tc * 512:(tc + 1) * 512], in_=bank(b), func=AF.Silu),
                     reads=[bk(b)], writes=[("GT", tc)])
            else:
                P.op("act", lambda e, b=b, tc=tc: e.activation(out=VT[:, tc * 512:(tc + 1) * 512], in_=bank(b), func=AF.Copy),
                     reads=[bk(b)], writes=[("VT", tc)])

        P0_WORDS = 2 * 4096 + 4 * 1024 + 4 * 512 + 512 + 2 * 512
        assert AR.off <= ARW - P0_WORDS or True
        AR.off = ARW - P0_WORDS
        wa = [AR.alloc(4096, F32) for _ in range(2)]
        P.op("act", lambda e: e.activation(out=scs, in_=cTs, func=AF.Silu), reads=["cTs"], writes=["scs"])
        accm = [AR.alloc(512, F32) for _ in range(2)]
        for g in range(6):
            P.dma("sp", lambda e, g=g: e.dma_start(out=wa[g % 2], in_=wada[g]), f"wa{g % 2}", writes=[("wa", g % 2)])
            for kc in range(8):
                if kc == 0:
                    P.op("dve", lambda e, g=g: e.tensor_scalar(out=accm[g % 2], in0=wa[g % 2][:, 0:512], scalar1=scs[:, 0:1],
                                                               scalar2=None, op0=ALU.mult),
                         reads=[("wa", g % 2), "scs"], writes=[("accm", g % 2)])
                else:
                    P.op("dve", lambda e, g=g, kc=kc: e.scalar_tensor_tensor(
                        out=accm[g % 2], in0=wa[g % 2][:, kc * 512:(kc + 1) * 512], scalar=scs[:, kc:kc + 1], in1=accm[g % 2],
                        op0=ALU.mult, op1=ALU.add),
                        reads=[("wa", g % 2), "scs", ("accm", g % 2)], writes=[("accm", g % 2)])
            for j in range(4):
                col = g * 4 + j
                P.op("pe", lambda e, g=g, j=j, col=col: e.matmul(
                    bank(7, col, col + 1), lhsT=accm[g % 2][:, j * 128:(j + 1) * 128], rhs=onesf[:, 0:1], start=True, stop=True),
                    reads=[("accm", g % 2), "onesf"], writes=[bk(7)], signal=(j == 3))
        P.op("dve", lambda e: e.tensor_tensor(out=modT, in0=bank(7, 0, 24), in1=badas, op=ALU.add),
             reads=[bk(7), "badas"], writes=["modT"])
        P.op("dve", lambda e: e.scalar_tensor_tensor(out=g1T, in0=scaleT, scalar=1.0, in1=ngTs, op0=ALU.add, op1=ALU.mult),
             reads=["modT", "ngTs"], writes=["g1T"])

        if stop_after == "0a":
            P.op("dve", lambda e: e.tensor_copy(out=hT[:, 0:32], in_=small[:, 48:80]), reads=["modT", "g1T"], writes=["hTdbg"])
            finish_debug(hT[:, 0:32])
            return nc
        xt = [AR.alloc(1024, F32) for _ in range(4)]
        xn = [AR.alloc(1024, BF16) for _ in range(4)]
        junk = AR.alloc(1024, BF16)
        ss = stat[:, 0:NT]
        sq = stat[:, NT:2 * NT]
        rstd = stat[:, 2 * NT:3 * NT]

        def pT(s_, kc, a, b):
            base = (4 * s_ + kc // 2) * 1024 + (kc % 2) * 512
            return psum_bf[:, base + a:base + b]

        for tg in range(NC5):
            s_ = 1
            for i in range(4):
                tt = tg * 4 + i
                P.dma("sp", lambda e, tt=tt, i=i: e.dma_start(out=xt[i], in_=x[tt * 128:(tt + 1) * 128, :]),
                      f"xt{i}", writes=[("xt", i)])
                P.op("act", lambda e, tt=tt, i=i: e.activation(out=junk, in_=xt[i], func=AF.Square, accum_out=ss[:, tt:tt + 1]),
                     reads=[("xt", i)], writes=[("ss", tt), "junk"])
            P.op("act", lambda e, tg=tg: e.activation(out=sq[:, tg * 4:tg * 4 + 4], in_=ss[:, tg * 4:tg * 4 + 4], func=AF.Sqrt,
                                                      bias=epsc, scale=1.0 / D),
                 reads=[("ss", tg * 4 + i) for i in range(4)] + ["epsc"], writes=[("sq", tg)])
            P.op("dve", lambda e, tg=tg: e.reciprocal(out=rstd[:, tg * 4:tg * 4 + 4], in_=sq[:, tg * 4:tg * 4 + 4]),
                 reads=[("sq", tg)], writes=[("rstd", tg)])
            for i in range(4):
                tt = tg * 4 + i
                P.op("act", lambda e, tt=tt, i=i: e.activation(out=xn[i], in_=xt[i], func=AF.Copy, scale=rstd[:, tt:tt + 1]),
                     reads=[("xt", i), ("rstd", tg)], writes=[("xn", i)])
                for kc in range(8):
                    P.op("pe", lambda e, s_=s_, kc=kc, i=i: e.transpose(pT(s_, kc, i * 128, (i + 1) * 128),
                                                                         xn[i][:, kc * 128:(kc + 1) * 128], ident[:]),
                         reads=[("xn", i), "ident"], writes=[bk(4 * s_ + kc // 2)], signal=(kc == 7))
            for kc in range(8):
                P.op("dve", lambda e, s_=s_, kc=kc, tg=tg: e.tensor_scalar(
                    out=hT3[:, kc, tg * 512:(tg + 1) * 512], in0=pT(s_, kc, 0, 512),
                    scalar1=g1T[:, kc:kc + 1], scalar2=shiftT[:, kc:kc + 1], op0=ALU.mult, op1=ALU.add),
                    reads=[bk(4 * s_ + kc // 2), "g1T", "modT"], writes=[("hT", tg)])
            if tg >= 1:
                for which in range(4):
                    proj_chunk(which, tg - 1, 0)
        for which in range(4):
            proj_chunk(which, NC5 - 1, 0)
        all_hT = [("hT", tg) for tg in range(NC5)]

        P.op("pool", lambda e: e.memset(MnT, 0.0), reads=all_hT, writes=["MnT"])
        P.op("pool", lambda e: e.memset(V[:, 0:NT * 129], 1.0), reads=all_hT, writes=[("V", tg) for tg in range(NC5)])

        for h in range(H):
            ws = h % 2
            wq, wk, wv, wg = (wr(ws, i) for i in range(4))
            wkeys = [("wr", ws, i) for i in range(4)]
            ns = (h + 1) % 2
            if h + 1 < H:
                for i, cb in enumerate((h + 1, 8 + h + 1, 16 + h + 1, 24 + h + 1)):
                    load_w(ns, i, cb)
            else:
                for i, cb in enumerate((32, 40, 48, 56)):
                    load_w(ns, i, cb)
            P.dma("sp", lambda e, h=h: e.dma_start(out=BT[h % 2], in_=btd[h]), f"bt{h % 2}",
                  reads=(all_hT if h == 0 else []), writes=[("BT", h % 2)])
            def proj_fm(which):
                if h == 0:
                    return
                for tc in range(NC5):
                    proj_chunk(which, tc, ws)

            proj_fm(0)
            proj_fm(1)
            if h > 0:
                proj_chunk(2, 0, ws)
            P.op("dve", lambda e: e.tensor_scalar(out=kmT[:, 0:NB], in0=kms[:, 0:NB], scalar1=1.0 / 256, scalar2=None, op0=ALU.mult),
                 reads=[("kms", tc) for tc in range(NC5)] + ["kmT"], writes=["kmT"])
            scb = nextbank()
            for qt in range(NT):
                P.op("pe", lambda e, qt=qt, scb=scb: e.matmul(bank(scb, qt * 16, (qt + 1) * 16), lhsT=QT[:, qt * 128:(qt + 1) * 128],
                                                              rhs=kmT[:, 0:16], start=True, stop=True),
                     reads=[("QT", qt // 4), "kmT"], writes=[bk(scb)], signal=(qt == NT - 1))
            P.op("dve", lambda e, scb=scb: e.tensor_tensor(out=scm, in0=bank(scb, 0, NT * 16), in1=pmask[:], op=ALU.add),
                 reads=[bk(scb), "pmask"], writes=["scm"])
            for qt in range(8, NT):
                P.op("dve", lambda e, qt=qt: e.max(out=top83[:, qt, :], in_=scm[:, qt * 16:(qt + 1) * 16]),
                     reads=["scm"], writes=[("top8", qt)])
            scm3 = scm.rearrange("p (q j) -> p q j", j=16)
            ltb3 = ltb.rearrange("p (q j) -> p q j", j=16)
            P.op("dve", lambda e: e.tensor_tensor(out=ltb3, in0=scm3, in1=top83[:, :, 2:3].to_broadcast([128, NT, 16]), op=ALU.is_lt),
                 reads=["scm", "top8"] + [("top8", qt) for qt in range(8, NT)], writes=["ltb"])
            P.op("dve", lambda e: e.tensor_scalar(out=mneg, in0=ltb, scalar1=NEG, scalar2=None, op0=ALU.mult),
                 reads=["ltb"], writes=["mneg"])
            if h > 0:
                for tc in range(1, NC5):
                    proj_chunk(2, tc, ws)
            proj_fm(3)
            nbk = S // 1024
            for qt in range(NT):
                P.op("pe", lambda e, qt=qt: e.transpose(psum_bf[0:16, 4096 + qt * 128:4096 + (qt + 1) * 128], mneg3[:, qt, :], ident[:]),
                     reads=["mneg", "ident"], writes=[bk(4 + qt // 8)], signal=(qt % 8 == 7))
            for b in range(nbk):
                P.op("dve", lambda e, b=b: e.tensor_copy(out=MnT[0:16, b * 1024:(b + 1) * 1024],
                                                         in_=psum_bf[0:16, 4096 + b * 1024:4096 + (b + 1) * 1024]),
                     reads=[bk(4 + b)], writes=["MnT"])
            for tg in range(NC5):
                b = nextbank()
                for i in range(4):
                    tt = tg * 4 + i
                    P.op("pe", lambda e, b=b, i=i, tt=tt: e.transpose(psum_bf[:, b * 1024 + i * 128:b * 1024 + (i + 1) * 128],
                                                                      VT[:, tt * 128:(tt + 1) * 128], ident[:]),
                         reads=[("VT", tg), "ident"], writes=[bk(b)], signal=(i == 3))
                P.op("dve", lambda e, b=b, tg=tg: e.tensor_copy(
                    out=V3[:, tg * 4:(tg + 1) * 4, 0:128],
                    in_=psum_bf[:, b * 1024:b * 1024 + 512].rearrange("p (t d) -> p t d", d=128)),
                     reads=[bk(b)], writes=[("V", tg)])
            if stop_after == "A1":
                finish_debug(arena_t[:, 0:2 * S].bitcast(BF16))
                return nc
            if stop_after == "A2":
                finish_debug(MnT)
                return nc
            steps = [(qb, jb) for qb in range(NB) for jb in range(qb + 1)]
            LAG = 3
            bts = BT[h % 2]

            def qk(i):
                qb, jb = steps[i]
                sbk = i - (i // 4) * 4
                for kt in range(2):
                    P.op("pe", lambda e, kt=kt, qb=qb, jb=jb, sbk=sbk: e.matmul(
                        bank(sbk, kt * 256, (kt + 1) * 256), lhsT=KT[:, jb * 256 + kt * 128:jb * 256 + (kt + 1) * 128],
                        rhs=QT[:, qb * 256:(qb + 1) * 256], start=True, stop=(jb == qb or qb <= 3)),
                        reads=[("KT", jb // 2), ("QT", qb // 2)], writes=[bk(sbk)], signal=((jb == qb or qb <= 3) and kt == 1))
                    if jb < qb and qb > 3:
                        P.op("pe", lambda e, kt=kt, qb=qb, jb=jb, sbk=sbk: e.matmul(
                            bank(sbk, kt * 256, (kt + 1) * 256), lhsT=Emat3[:, jb, :],
                            rhs=MnT[:, qb * 256:(qb + 1) * 256], start=False, stop=True),
                            reads=["Emat", "MnT"], writes=[bk(sbk)], signal=(kt == 1))
                if jb >= qb - 1:
                    off = 0 if jb == qb else 512
                    P.op("dve", lambda e, sbk=sbk, off=off, bts=bts: e.tensor_tensor(out=bank(sbk), in0=bank(sbk), in1=bts[:, off:off + 512],
                                                                            op=ALU.add),
                         reads=[bk(sbk), ("BT", h % 2)], writes=[bk(sbk)])
                    P.op("act", lambda e, sbk=sbk, i=i: e.activation(out=Pt[i % NPT], in_=bank(sbk), func=AF.Exp),
                         reads=[bk(sbk)], writes=[("Pt", i % NPT)])
                else:
                    P.op("act", lambda e, sbk=sbk, i=i, h=h: e.activation(out=Pt[i % NPT], in_=bank(sbk), func=AF.Exp,
                                                                     bias=c31s[:, h:h + 1]),
                         reads=[bk(sbk), "c31s"], writes=[("Pt", i % NPT)])

                a2 = qb % 2
                if i in pool_steps[qb]:
                    if i == pool_steps[qb][0]:
                        P.op("pool", lambda e, i=i, a2=a2: e.tensor_copy(out=accP[a2], in_=Pt[i % NPT]),
                             reads=[("Pt", i % NPT)], writes=[("acc", a2)])
                    else:
                        P.op("pool", lambda e, i=i, a2=a2: e.tensor_tensor(out=accP[a2], in0=accP[a2], in1=Pt[i % NPT], op=ALU.add),
                             reads=[("Pt", i % NPT), ("acc", a2)], writes=[("acc", a2)])
                    if i == pool_steps[qb][-1]:
                        P.op("pool", lambda e, a2=a2: e.tensor_tensor(out=accF[a2], in0=accP[a2][:, 0:256], in1=accP[a2][:, 256:512],
                                                                      op=ALU.add),
                             reads=[("acc", a2)], writes=[("accF", a2)])
                        P.op("pool", lambda e, a2=a2: e.tensor_copy(out=dhi[a2], in_=accF[a2]),
                             reads=[("accF", a2)], writes=[("dhi", a2)])
                        P.op("pool", lambda e, a2=a2: e.tensor_tensor(out=dlo[a2], in0=accF[a2], in1=dhi[a2], op=ALU.subtract),
                             reads=[("accF", a2), ("dhi", a2)], writes=[("dlo", a2)])

            def obank(qb, qt2):
                return 4 + 2 * (qb % 2) + qt2

            def ocol(qt2):
                return 0

            def pv(i):
                qb, jb = steps[i]
                for qt2 in range(2):
                    ob = obank(qb, qt2)
                    for kt in range(2):
                        if jb == qb and kt == 1 and qt2 == 0:
                            continue
                        first = (jb == 0 and kt == 0)
                        last = (jb == qb and kt == 1) or (jb == qb and kt == 0 and qt2 == 0)
                        oc = ocol(qt2)
                        P.op("pe", lambda e, kt=kt, jb=jb, i=i, ob=ob, qt2=qt2, first=first, last=last, oc=oc: e.matmul(
                            bank(ob, oc, oc + 129), lhsT=Pt[i % NPT][:, kt * 256 + qt2 * 128:kt * 256 + (qt2 + 1) * 128],
                            rhs=V3[:, jb * 2 + kt, :], start=first, stop=last),
                            reads=[("V", jb // 2), ("Pt", i % NPT)], writes=[bk(ob)], signal=(qt2 == 1 and kt == 1))

            def finalize_a(qb):
                for qt2 in range(2):
                    ob = obank(qb, qt2)
                    a_ = at_[2 * (qb % 2) + qt2]
                    rc = rden[:, 2 * (qb % 2) + qt2:2 * (qb % 2) + qt2 + 1]
                    akey = ("at", 2 * (qb % 2) + qt2)
                    oc = ocol(qt2)
                    P.op("dve", lambda e, ob=ob, rc=rc, oc=oc: e.reciprocal(out=rc, in_=bank(ob, oc + 128, oc + 129)),
                         reads=[bk(ob)], writes=[("rden", qb % 2, qt2)])
                    P.op("dve", lambda e, ob=ob, rc=rc, a_=a_, oc=oc: e.tensor_scalar(out=a_, in0=bank(ob, oc, oc + 128), scalar1=rc, scalar2=None,
                                                                             op0=ALU.mult),
                         reads=[bk(ob), ("rden", qb % 2, qt2)], writes=[akey])

            def finalize(qb):
                tb0 = 0
                fb = obank(qb, 0)
                for qt2 in range(2):
                    a_ = at_[2 * (qb % 2) + qt2]
                    akey = ("at", 2 * (qb % 2) + qt2)
                    P.op("pe", lambda e, a_=a_, qt2=qt2, tb0=tb0, fb=fb: e.transpose(bank(fb, tb0 + qt2 * 128, tb0 + (qt2 + 1) * 128), a_,
                                                                                      identf[:]),
                         reads=[akey, "identf"], writes=[bk(fb)], signal=(qt2 == 1))
                P.op("dve", lambda e, qb=qb, tb0=tb0, h=h, fb=fb: e.tensor_tensor(out=aGh[h % 2][:, qb * 256:(qb + 1) * 256],
                                                                         in0=bank(fb, tb0, tb0 + 256),
                                                                         in1=GT[:, qb * 256:(qb + 1) * 256], op=ALU.mult),
                     reads=[bk(fb), ("GT", qb // 2)], writes=[("aGh", h % 2)])

            n = len(steps)
            pool_steps = {qb: [] for qb in range(NB)}
            pe_steps = {qb: [] for qb in range(NB)}
            for i_, (qb_, jb_) in enumerate(steps):
                (pool_steps if (POOL_DEN and i_ % POOL_DEN == POOL_DEN - 1) else pe_steps)[qb_].append(i_)
            pend = []
            for i in range(n + LAG):
                if i < n:
                    qk(i)
                if i - LAG >= 0:
                    if steps[i - LAG][1] == 0:
                        while pend and pend[0][0] <= steps[i - LAG][0] - 2:
                            finalize(pend.pop(0)[0])
                    pv(i - LAG)
                    while pend and pend[0][1] <= i:
                        finalize(pend.pop(0)[0])
                    if steps[i - LAG][1] == steps[i - LAG][0]:
                        finalize_a(steps[i - LAG][0])
                        pend.append((steps[i - LAG][0], i + 3))
            for qb_, _ in pend:
                finalize(qb_)
            P.dma("sp", lambda e, h=h: e.dma_start(out=aGd[h], in_=aGh[h % 2]), f"aGd{h % 2}",
                  reads=[("aGh", h % 2)], writes=[("aGd", h)])
            if stop_after == "A3":
                finish_debug(aGh[0])
                return nc
        P.barrier()

        AR.reset()
        PB_WORDS = ((S + 2 + 7) // 8 * 8) + 8 * 512 + 2 * (S // 2)
        AR.off = (ARW - PB_WORDS) // 8 * 8
        N_PRE = min(32, AR.off // 512)
        order = []
        for j in range(8):
            order += [(j, 80 + j), (8 + j, 88 + j), (16 + j, 64 + j), (24 + j, 72 + j)]
        Wslot = {wi: arena_t[:, k * 512:(k + 1) * 512].bitcast(BF16) for k, (wi, cb) in enumerate(order)}

        def load_wj(k):
            wi, cb = order[k]
            P.dma("pool", lambda e, wi=wi, cb=cb: e.dma_start(out=Wslot[wi], in_=wall[cb]), f"wj{wi}", writes=[("Wj", wi)])

        u = AR.alloc(S + 2, F32)
        cxs = [AR.alloc(512, F32) for _ in range(2)]
        ybuf = [AR.alloc(512, F32) for _ in range(2)]
        sgb = [AR.alloc(512, F32) for _ in range(2)]
        zb = [AR.alloc(512, F32) for _ in range(2)]
        cGs = [AR.alloc(S, BF16) for _ in range(2)]
        P.op("pool", lambda e: e.memset(u[:, 0:2], 0.0), writes=[("u", -1)])
        for c in range(8):
            ws = c % 2
            wcb, wcc, wcx, wgc = (wr(ws, i) for i in range(4))
            wkeys = [("wr", ws, i) for i in range(4)]
            if c + 1 < 8:
                for i, cb in enumerate((32 + c + 1, 40 + c + 1, 48 + c + 1, 56 + c + 1)):
                    load_w((c + 1) % 2, i, cb)
            if c == 1:
                for k in range(N_PRE):
                    load_wj(k)
            for tc in range(NC5):
                for wi, w_ in enumerate((wcb, wcc, wcx, wgc)):
                    for kc in range(8):
                        P.op("pe", lambda e, wi=wi, w_=w_, kc=kc, tc=tc: e.matmul(
                            bank(wi + 4 * (tc % 2)), lhsT=w_[:, kc * 128:(kc + 1) * 128], rhs=hT3[:, kc, tc * 512:(tc + 1) * 512],
                            start=(kc == 0), stop=(kc == 7)),
                            reads=[wkeys[wi], "hT"], writes=[bk(wi + 4 * (tc % 2))], signal=(kc == 7))
                o = 4 * (tc % 2)
                k2 = tc % 2
                P.op("act", lambda e, o=o, k2=k2: e.activation(out=cxs[k2], in_=bank(o + 2), func=AF.Copy),
                     reads=[bk(o + 2)], writes=[("cxs", k2)])
                P.op("dve", lambda e, o=o, k2=k2, tc=tc: e.tensor_tensor(out=u[:, 2 + tc * 512:2 + (tc + 1) * 512], in0=bank(o + 1),
                                                                         in1=cxs[k2], op=ALU.mult),
                     reads=[bk(o + 1), ("cxs", k2)], writes=[("u", tc)])
                P.op("dve", lambda e, k2=k2, tc=tc, c=c: e.tensor_scalar(out=ybuf[k2], in0=u[:, 2 + tc * 512:2 + (tc + 1) * 512],
                                                                         scalar1=cws[:, c * 3 + 2:c * 3 + 3], scalar2=None, op0=ALU.mult),
                     reads=[("u", tc), "cws"], writes=[("y", k2)])
                P.op("dve", lambda e, k2=k2, tc=tc, c=c: e.scalar_tensor_tensor(
                    out=ybuf[k2], in0=u[:, 1 + tc * 512:1 + (tc + 1) * 512], scalar=cws[:, c * 3 + 1:c * 3 + 2], in1=ybuf[k2],
                    op0=ALU.mult, op1=ALU.add),
                    reads=[("u", tc), ("u", tc - 1), ("y", k2)], writes=[("y", k2)])
                P.op("dve", lambda e, k2=k2, tc=tc, c=c: e.scalar_tensor_tensor(
                    out=ybuf[k2], in0=u[:, tc * 512:(tc + 1) * 512], scalar=cws[:, c * 3:c * 3 + 1], in1=ybuf[k2],
                    op0=ALU.mult, op1=ALU.add),
                    reads=[("u", tc), ("u", tc - 1), ("y", k2)], writes=[("y", k2)])
                P.op("act", lambda e, o=o, k2=k2: e.activation(out=sgb[k2], in_=bank(o + 3), func=AF.Silu),
                     reads=[bk(o + 3)], writes=[("sg", k2)])
                P.op("dve", lambda e, o=o, k2=k2: e.tensor_tensor(out=zb[k2], in0=bank(o), in1=ybuf[k2], op=ALU.mult),
                     reads=[bk(o), ("y", k2)], writes=[("z", k2)])
                P.op("dve", lambda e, k2=k2, tc=tc, c=c: e.tensor_tensor(out=cGs[c % 2][:, tc * 512:(tc + 1) * 512], in0=zb[k2],
                                                                         in1=sgb[k2], op=ALU.mult),
                     reads=[("z", k2), ("sg", k2)], writes=[("cGs", c % 2)])
            P.dma("sp", lambda e, c=c: e.dma_start(out=cGd[c], in_=cGs[c % 2]), f"cGd{c % 2}",
                  reads=[("cGs", c % 2)], writes=[("cGd", c)])
        P.barrier()

        AR.reset()
        AR.off = 32 * 512
        Wj = [Wslot[wi] for wi in range(32)]
        aGc = [AR.alloc(8 * 256, BF16) for _ in range(2)]
        cGc = [AR.alloc(8 * 256, BF16) for _ in range(2)]
        sgm = [AR.alloc(512, F32) for _ in range(2)]
        tmp = [AR.alloc(512, F32) for _ in range(2)]
        mst = [AR.alloc(8 * 256, BF16) for _ in range(2)]
        for k in range(N_PRE, 32):
            load_wj(k)
        Wg3 = wring[:].rearrange("p (k n) -> p k n", k=8)
        gbc = AR.alloc(1024, F32)
        wst1 = AR.alloc(1024, F32)
        for kc in range(8):
            P.op("dve", lambda e, kc=kc: e.tensor_scalar(out=wst1[:, kc * 128:(kc + 1) * 128], in0=identf[:], scalar1=gateT[:, kc:kc + 1],
                                                         scalar2=None, op0=ALU.mult),
                 writes=[("Dk", kc)])
            P.op("pe", lambda e, kc=kc: e.matmul(bank(4 + kc // 4, (kc % 4) * 128, (kc % 4 + 1) * 128), lhsT=onesf[:],
                                                 rhs=wst1[:, kc * 128:(kc + 1) * 128], start=True, stop=True),
                 reads=[("Dk", kc)], writes=[bk(4 + kc // 4)], signal=(kc % 4 == 3))
        P.op("dve", lambda e: e.tensor_copy(out=gbc, in_=psum[:, 2048:3072]), reads=[bk(4), bk(5)], writes=["gbc"])
        wg_done = [0]

        def wg_piece(kc):
            P.dma("sp", lambda e, kc=kc: e.dma_start(out=wst1, in_=wout[:, kc * 1024:(kc + 1) * 1024]), "wst1",
                  writes=["wst1"] + [("Dk", k_) for k_ in range(8)])
            P.op("dve", lambda e, kc=kc: e.tensor_tensor(out=Wg3[:, kc, :], in0=wst1, in1=gbc, op=ALU.mult),
                 reads=["wst1", "gbc"], writes=["Wg"])
        aGd_v = aGd.rearrange("h p s -> p h s")
        cGd_v = cGd.rearrange("h p s -> p h s")
        for t in range(NCH):
            k2 = t % 2
            a3 = aGc[k2].rearrange("p (h s) -> p h s", h=8)
            c3 = cGc[k2].rearrange("p (h s) -> p h s", h=8)
            m3 = mst[k2].rearrange("p (h s) -> p h s", h=8)
            P.dma("sp", lambda e, t=t, a3=a3: e.dma_start(out=a3, in_=aGd_v[:, :, t * 256:(t + 1) * 256]), f"aGc{k2}",
                  writes=[("aGc", k2)])
            P.dma("sp", lambda e, t=t, c3=c3: e.dma_start(out=c3, in_=cGd_v[:, :, t * 256:(t + 1) * 256]), f"cGc{k2}",
                  writes=[("cGc", k2)])
            if 2 <= t < 10:
                wg_piece(t - 2)
                wg_done[0] = t - 1
            for j in range(8):
                idx = t * 8 + j
                bA, bB = 2 * (idx % 2), 2 * (idx % 2) + 1
                groups = [(bA, 0, Wj[j], ("Wj", j), a3, ("aGc", k2)),
                          (bA, 256, Wj[8 + j], ("Wj", 8 + j), c3, ("cGc", k2)),
                          (bB, 0, Wj[16 + j], ("Wj", 16 + j), None, None),
                          (bB, 256, Wj[24 + j], ("Wj", 24 + j), None, None)]
                for gi, (bb, off, w_, wkey, src, skey) in enumerate(groups):
                    for kc in range(8):
                        if src is None:
                            rhs = hT3[:, kc, t * 256:(t + 1) * 256]
                            rk = ("hT", t)
                        else:
                            rhs = src[:, kc, :]
                            rk = skey
                        P.op("pe", lambda e, bb=bb, off=off, w_=w_, kc=kc, rhs=rhs: e.matmul(
                            bank(bb, off, off + 256), lhsT=w_[:, kc * 128:(kc + 1) * 128], rhs=rhs,
                            start=(kc == 0), stop=(kc == 7)),
                            reads=[wkey, rk], writes=[bk(bb)], signal=(kc == 7 and gi % 2 == 1))
                i2 = idx % 2
                P.op("act", lambda e, bB=bB, i2=i2: e.activation(out=sgm[i2], in_=bank(bB), func=AF.Sigmoid),
                     reads=[bk(bB)], writes=[("sgm", i2)])
                P.op("dve", lambda e, bA=bA, i2=i2: e.tensor_tensor(out=tmp[i2], in0=bank(bA), in1=sgm[i2], op=ALU.mult),
                     reads=[bk(bA), ("sgm", i2)], writes=[("tmp", i2)])
                P.op("dve", lambda e, i2=i2, j=j, m3=m3: e.tensor_tensor(out=m3[:, j, :], in0=tmp[i2][:, 0:256], in1=tmp[i2][:, 256:512],
                                                                         op=ALU.add),
                     reads=[("tmp", i2)], writes=[("mst", k2)])
            P.op("pool", lambda e, t=t, m3=m3: e.tensor_copy(out=hT3[:, :, t * 256:(t + 1) * 256], in_=m3),
                 reads=[("mst", k2)], writes=[("hT", t)])
        for kc in range(wg_done[0], 8):
            wg_piece(kc)
        P.barrier()

        AR.reset()
        xt2 = [AR.alloc(1024, F32) for _ in range(4)]
        rb = [AR.alloc(1024, F32) for _ in range(3)]
        ot = [AR.alloc(1024, F32) for _ in range(4)]
        junk2 = AR.alloc(1024, BF16)
        ss2 = stat[:, 0:NT]
        sq2 = stat[:, NT:2 * NT]
        rstd2 = stat[:, 2 * NT:3 * NT]
        out_keys = []
        def load_x2(tt):
            P.dma("sp", lambda e, tt=tt: e.dma_start(out=xt2[tt % 4], in_=x[tt * 128:(tt + 1) * 128, :]), f"xt2{tt % 4}",
                  writes=[("xt2", tt % 4)])

        PF = 3
        for tt in range(min(PF, NT)):
            load_x2(tt)
        for tt in range(NT):
            k4 = tt % 4
            k3 = tt % 3
            if tt + PF < NT:
                load_x2(tt + PF)
            for half in range(2):
                bb = 2 * k4 + half
                for kc in range(8):
                    P.op("pe", lambda e, bb=bb, kc=kc, tt=tt, half=half: e.matmul(
                        bank(bb), lhsT=hT3[:, kc, tt * 128:(tt + 1) * 128], rhs=Wg3[:, kc, half * 512:(half + 1) * 512],
                        start=(kc == 0), stop=(kc == 7)),
                        reads=[], writes=[bk(bb)], signal=(kc == 7 and half == 1))
            P.op("dve", lambda e, k4=k4, k3=k3: e.tensor_tensor(out=rb[k3], in0=psum[:, (2 * k4) * 512:(2 * k4) * 512 + 1024],
                                                         in1=xt2[k4], op=ALU.add),
                 reads=[bk(2 * k4), bk(2 * k4 + 1), ("xt2", k4)], writes=[("rb", k3)])
            P.op("act", lambda e, k3=k3, tt=tt: e.activation(out=junk2, in_=rb[k3], func=AF.Square, accum_out=ss2[:, tt:tt + 1]),
                 reads=[("rb", k3)], writes=[("ss2", tt), "junk2"])
            P.op("act", lambda e, tt=tt: e.activation(out=sq2[:, tt:tt + 1], in_=ss2[:, tt:tt + 1], func=AF.Sqrt, bias=epsc,
                                                      scale=1.0 / D),
                 reads=[("ss2", tt), "epsc"], writes=[("sq2", tt)])
            P.op("dve", lambda e, tt=tt: e.reciprocal(out=rstd2[:, tt:tt + 1], in_=sq2[:, tt:tt + 1]),
                 reads=[("sq2", tt)], writes=[("rstd2", tt)])
            P.op("dve", lambda e, k3=k3, k4=k4, tt=tt: e.scalar_tensor_tensor(out=ot[k4], in0=rb[k3], scalar=rstd2[:, tt:tt + 1], in1=fgb[:],
                                                                       op0=ALU.mult, op1=ALU.mult),
                 reads=[("rb", k3), ("rstd2", tt), "fgb"], writes=[("ot", k4)])
            P.dma("sp", lambda e, tt=tt, k4=k4: e.dma_start(out=out[tt * 128:(tt + 1) * 128, :], in_=ot[k4]), f"out{k4}",
                  reads=[("ot", k4)], writes=[("out", tt)])
            out_keys.append(("out", tt))
        P.barrier()
        P.emit(st)
    return nc


def _bucket_table(n):
    d = np.arange(n)
    nf = np.maximum(d, 1).astype(np.float32)
    large = 16 + (np.log(nf / np.float32(16)) / np.float32(np.log(128 / 16)) * np.float32(16)).astype(np.int32)
    large = np.minimum(large, 31)
    return np.where(d < 16, d, large)


def prep_shared(norm_g, w_ada, b_ada, w_in, conv_w, w_o_attn, w_o_conv, w_out, rel_bias, final_g):
    f = np.float32
    sh = {}
    sh["ngT"] = np.ascontiguousarray(norm_g[0].reshape(8, 128).T).astype(f)
    sh["badaT"] = np.ascontiguousarray(b_ada[0].reshape(24, 128).T).astype(f)
    sh["wada"] = np.ascontiguousarray(w_ada[0].reshape(8, 128, 6, 512).transpose(2, 1, 0, 3)).reshape(6, 128, 4096)

    def colblocks(w):
        ncb = w.shape[1] // 128
        return np.ascontiguousarray(w.reshape(8, 128, ncb, 128).transpose(2, 1, 0, 3)).reshape(ncb, 128, 1024)

    sh["wall"] = np.concatenate([colblocks(w_in[0]), colblocks(w_o_attn[0]), colblocks(w_o_conv[0])], axis=0)
    sh["wout"] = np.ascontiguousarray(w_out[0].reshape(8, 128, 1024).transpose(1, 0, 2)).reshape(128, 8192)
    sh["convw"] = np.ascontiguousarray(conv_w[0].T.reshape(8, 128, 3).transpose(1, 0, 2)).reshape(128, 24)
    sh["c31"] = np.ascontiguousarray(rel_bias[31:32, :]).astype(f)
    sh["fg"] = np.ascontiguousarray(final_g.reshape(1, 1024)).astype(f)
    bkt = _bucket_table(512)
    k = np.arange(256)[:, None]
    q = np.arange(256)[None, :]
    d_own = q - k
    d_prev = q + 256 - k
    bt = np.empty((8, 128, 1024), f)
    for h in range(8):
        tb = rel_bias[:, h]
        own = np.where(d_own >= 0, tb[bkt[np.maximum(d_own, 0)]], f(NEG)).astype(f)
        prev = tb[bkt[d_prev]].astype(f)
        bt[h, :, 0:512] = own.reshape(2, 128, 256).transpose(1, 0, 2).reshape(128, 512)
        bt[h, :, 512:1024] = prev.reshape(2, 128, 256).transpose(1, 0, 2).reshape(128, 512)
    sh["bt"] = bt
    return sh


_CACHE = {}


def _get_prog(S):
    if S not in _CACHE:
        _CACHE[S] = build_program(S)
    return _CACHE[S]


def kernel(x, c, norm_g, w_ada, b_ada, w_in, conv_w, w_o_attn, w_o_conv, w_out, rel_bias, final_g):
    x = np.asarray(x, np.float32)
    c = np.asarray(c, np.float32)
    B, S, _ = x.shape
    sh = prep_shared(*(np.asarray(a, np.float32) for a in
                       (norm_g, w_ada, b_ada, w_in, conv_w, w_o_attn, w_o_conv, w_out, rel_bias, final_g)))
    nc = _get_prog(S)
    in_maps = []
    for b in range(B):
        m = dict(sh)
        m["x"] = np.ascontiguousarray(x[b])
        m["cT"] = np.ascontiguousarray(c[b].reshape(8, 128).T)
        in_maps.append(m)
    res = run_bass_kernel_spmd(nc, in_maps, core_ids=list(range(B)))
    return np.stack([np.asarray(r["out"], np.float32) for r in res.results], axis=0)
```
